# Optimizing a Trainium2 kernel written in Bass

```python
import math
import jax, jax.numpy as jnp
from jax import lax
import numpy as np

D_MODEL = 1024
BATCH = 2
SEQ = 8192
DEPTH = 1

D_RNN = D_MODEL // 2
N_RNN_BLOCKS = 8
RNN_BLOCK = D_RNN // N_RNN_BLOCKS
CONV_WIDTH = 4
CONV_LEFT = 2
LRU_C = 8.0
N_ATTN_HEADS = 8
HEAD_DIM = 64
D_ATTN = N_ATTN_HEADS * HEAD_DIM
DILATED_PATTERNS = ((128, 1), (512, 4), (2048, 16))
Q_BLOCK = 128
N_BUCKETS = 32
MAX_DISTANCE = 1024
D_MIX = D_RNN + D_ATTN
D_IN = 2 * D_RNN + 3 * D_ATTN
D_FF = 4 * D_MODEL
EPS = 1e-6
NEG_INF = -1e30

kernel_name = "hybrid_rglru_dilated_attn_block"


def rms_norm(x, g):
    xf = x.astype(jnp.float32)
    y = xf * lax.rsqrt(jnp.mean(xf * xf, axis=-1, keepdims=True) + EPS)
    return (y * g.astype(jnp.float32)).astype(x.dtype)


def t5_bucket(rel):
    nb = N_BUCKETS // 2
    max_exact = nb // 2
    ret = jnp.where(rel > 0, nb, 0)
    n = jnp.abs(rel)
    nf = jnp.maximum(n, 1).astype(jnp.float32)
    large = max_exact + (jnp.log(nf / max_exact) / math.log(MAX_DISTANCE / max_exact)
                         * (nb - max_exact)).astype(jnp.int32)
    large = jnp.minimum(large, nb - 1)
    return ret + jnp.where(n < max_exact, n, large)


def centred_depthwise_conv(x, w, b):
    S = x.shape[1]
    xp = jnp.pad(x, ((0, 0), (CONV_LEFT, CONV_WIDTH - 1 - CONV_LEFT), (0, 0)))
    y = b
    for k in range(CONV_WIDTH):
        y = y + xp[:, k:k + S] * w[k]
    return y


def rg_lru(x, wa, ba, wx, bx, lam, reverse):
    B, S, _ = x.shape
    xb = x.reshape(B, S, N_RNN_BLOCKS, RNN_BLOCK)
    r = jax.nn.sigmoid((jnp.einsum('bsnc,ncd->bsnd', xb, wa).reshape(B, S, D_RNN) + ba).astype(jnp.float32))
    i = jax.nn.sigmoid((jnp.einsum('bsnc,ncd->bsnd', xb, wx).reshape(B, S, D_RNN) + bx).astype(jnp.float32))
    log_a = -LRU_C * jax.nn.softplus(-lam.astype(jnp.float32)) * r
    a = jnp.exp(log_a)
    b_in = jnp.sqrt(-jnp.expm1(2.0 * log_a)) * (i * x.astype(jnp.float32))

    def combine(c1, c2):
        a1, b1 = c1
        a2, b2 = c2
        return a1 * a2, a2 * b1 + b2

    _, h = lax.associative_scan(combine, (a, b_in), reverse=reverse, axis=1)
    return h


def dilated_attention(q, k, v, rel_bias):
    B, S, H, Dh = q.shape
    nb = S // Q_BLOCK
    scale = Dh ** -0.5
    pats = []
    for window, dil in DILATED_PATTERNS:
        half = window // (2 * dil)
        offs = jnp.arange(-half, half + 1, dtype=jnp.int32) * dil
        bias = rel_bias[t5_bucket(offs)].astype(jnp.float32).T
        pats.append((offs, bias))
    q_blocks = q.reshape(B, nb, Q_BLOCK, H, Dh).transpose(1, 0, 2, 3, 4)

    def one_block(args):
        qb, n = args
        pos = n * Q_BLOCK + jnp.arange(Q_BLOCK, dtype=jnp.int32)
        outs, lses = [], []
        for offs, bias in pats:
            kpos = pos[:, None] + offs[None, :]
            valid = (kpos >= 0) & (kpos < S)
            kidx = jnp.clip(kpos, 0, S - 1)
            kg = k[:, kidx]
            vg = v[:, kidx]
            logits = jnp.einsum('bqhd,bqjhd->bhqj', qb, kg).astype(jnp.float32) * scale
            logits = logits + bias[None, :, None, :]
            logits = jnp.where(valid[None, None], logits, NEG_INF)
            lse = jax.nn.logsumexp(logits, axis=-1)
            p = jnp.exp(logits - lse[..., None])
            outs.append(jnp.einsum('bhqj,bqjhd->bqhd', p.astype(v.dtype), vg).astype(jnp.float32))
            lses.append(lse)
        w = jax.nn.softmax(jnp.stack(lses, axis=0), axis=0)
        w = jnp.transpose(w, (0, 1, 3, 2))[..., None]
        o = jnp.sum(w * jnp.stack(outs, axis=0), axis=0)
        return o.astype(q.dtype)

    out = lax.map(one_block, (q_blocks, jnp.arange(nb, dtype=jnp.int32)))
    return out.transpose(1, 0, 2, 3, 4).reshape(B, S, H * Dh)


def setup_inputs(seed: int = 0) -> dict:
    key = jax.random.key(seed)
    ks = jax.random.split(key, 24)
    f32 = jnp.float32

    def nrm(k, shape, s):
        return jax.random.normal(k, shape, f32) * s

    def gain(k, shape):
        return 1.0 + 0.02 * jax.random.normal(k, shape, f32)

    def lam(k):
        u = jax.random.uniform(k, (DEPTH, D_RNN), f32, minval=0.9, maxval=0.999)
        s = u ** (1.0 / LRU_C)
        return jnp.log(s) - jnp.log1p(-s)

    return {
        "x": jax.random.normal(ks[0], (BATCH, SEQ, D_MODEL), f32),
        "attn_norm_g": gain(ks[1], (DEPTH, D_MODEL)),
        "w_in": nrm(ks[2], (DEPTH, D_MODEL, D_IN), D_MODEL ** -0.5),
        "conv_w": nrm(ks[3], (DEPTH, CONV_WIDTH, D_RNN), CONV_WIDTH ** -0.5),
        "conv_b": nrm(ks[4], (DEPTH, D_RNN), 0.01),
        "lru_wa_fwd": nrm(ks[5], (DEPTH, N_RNN_BLOCKS, RNN_BLOCK, RNN_BLOCK), RNN_BLOCK ** -0.5),
        "lru_ba_fwd": nrm(ks[6], (DEPTH, D_RNN), 0.01),
        "lru_wx_fwd": nrm(ks[7], (DEPTH, N_RNN_BLOCKS, RNN_BLOCK, RNN_BLOCK), RNN_BLOCK ** -0.5),
        "lru_bx_fwd": nrm(ks[8], (DEPTH, D_RNN), 0.01),
        "lru_lam_fwd": lam(ks[9]),
        "lru_wa_bwd": nrm(ks[10], (DEPTH, N_RNN_BLOCKS, RNN_BLOCK, RNN_BLOCK), RNN_BLOCK ** -0.5),
        "lru_ba_bwd": nrm(ks[11], (DEPTH, D_RNN), 0.01),
        "lru_wx_bwd": nrm(ks[12], (DEPTH, N_RNN_BLOCKS, RNN_BLOCK, RNN_BLOCK), RNN_BLOCK ** -0.5),
        "lru_bx_bwd": nrm(ks[13], (DEPTH, D_RNN), 0.01),
        "lru_lam_bwd": lam(ks[14]),
        "rel_bias": nrm(ks[15], (N_BUCKETS, N_ATTN_HEADS), 0.5),
        "norm_rnn_g": gain(ks[16], (DEPTH, D_RNN)),
        "norm_attn_g": gain(ks[17], (DEPTH, D_ATTN)),
        "w_out": nrm(ks[18], (DEPTH, D_MIX, D_MODEL), D_MIX ** -0.5),
        "mlp_norm_g": gain(ks[19], (DEPTH, D_MODEL)),
        "w_up": nrm(ks[20], (DEPTH, D_MODEL, D_FF), D_MODEL ** -0.5),
        "w_down": nrm(ks[21], (DEPTH, D_FF, D_MODEL), D_FF ** -0.5),
        "final_norm_g": gain(ks[22], (D_MODEL,)),
    }


def reference(x, attn_norm_g, w_in, conv_w, conv_b,
              lru_wa_fwd, lru_ba_fwd, lru_wx_fwd, lru_bx_fwd, lru_lam_fwd,
              lru_wa_bwd, lru_ba_bwd, lru_wx_bwd, lru_bx_bwd, lru_lam_bwd,
              rel_bias, norm_rnn_g, norm_attn_g, w_out,
              mlp_norm_g, w_up, w_down, final_norm_g):
    B, S, _ = x.shape
    for l in range(DEPTH):
        h = rms_norm(x, attn_norm_g[l])
        proj = h @ w_in[l]
        xr, gate, q, k, v = jnp.split(
            proj, [D_RNN, 2 * D_RNN, 2 * D_RNN + D_ATTN, 2 * D_RNN + 2 * D_ATTN], axis=-1)
        xr = centred_depthwise_conv(xr, conv_w[l], conv_b[l])
        h_f = rg_lru(xr, lru_wa_fwd[l], lru_ba_fwd[l], lru_wx_fwd[l], lru_bx_fwd[l], lru_lam_fwd[l], False)
        h_b = rg_lru(xr, lru_wa_bwd[l], lru_ba_bwd[l], lru_wx_bwd[l], lru_bx_bwd[l], lru_lam_bwd[l], True)
        y_rnn = (h_f + h_b).astype(x.dtype) * jax.nn.gelu(gate)
        qh = q.reshape(B, S, N_ATTN_HEADS, HEAD_DIM)
        kh = k.reshape(B, S, N_ATTN_HEADS, HEAD_DIM)
        vh = v.reshape(B, S, N_ATTN_HEADS, HEAD_DIM)
        y_attn = dilated_attention(qh, kh, vh, rel_bias)
        mix = jnp.concatenate([rms_norm(y_rnn, norm_rnn_g[l]), rms_norm(y_attn, norm_attn_g[l])], axis=-1)
        x = x + mix @ w_out[l]
        h = rms_norm(x, mlp_norm_g[l])
        x = x + jnp.square(jax.nn.relu(h @ w_up[l])) @ w_down[l]
    return rms_norm(x, final_norm_g)
```

```python
import math
from contextlib import ExitStack

import numpy as np
import concourse.bass as bass
import concourse.mybir as mybir
from concourse.bass_utils import run_bass_kernel_spmd

F32 = mybir.dt.float32
BF16 = mybir.dt.bfloat16
ALU = mybir.AluOpType
AF = mybir.ActivationFunctionType
AX = mybir.AxisListType

ENGS = ("pe", "act", "dve", "pool", "sp")
NEG = -30000.0
EPS = 1e-6
NCORES = 8


class Res:
    __slots__ = ("name", "w", "pw", "r", "dsem", "dcnt", "excl")

    def __init__(self, name):
        self.name = name
        self.excl = False
        self.w = None
        self.pw = []
        self.r = []
        self.dsem = None
        self.dcnt = 0


class Sched:
    def __init__(self, nc, stack):
        self.nc = nc
        self.stack = stack
        self.ops = {e: [] for e in ENGS}
        self.cnt = {e: 0 for e in ENGS}
        self.sem = {e: stack.enter_context(nc.semaphore("sem_" + e)) for e in ENGS}
        self.waited = {e: {} for e in ENGS}
        self.nsem = 0
        self.all_res = []

    def res(self, name):
        r = Res(name)
        self.all_res.append(r)
        return r

    def _need(self, eng, waits, tok, same_ok):
        if tok is None:
            return
        sem, val, peng = tok
        if peng == eng and same_ok:
            return
        if peng in self.cnt and val > self.cnt[peng]:
            raise RuntimeError("dependency on unsignaled op of %s from %s" % (peng, eng))
        key = id(sem)
        if self.waited[eng].get(key, 0) >= val:
            return
        self.waited[eng][key] = val
        waits[key] = (sem, val)

    def _deps(self, eng, reads, writes, pwrites=()):
        waits = {}
        for r in reads:
            self._need(eng, waits, r.w, same_ok=(eng == "pe"))
            for t in r.pw:
                self._need(eng, waits, t, same_ok=(eng == "pe"))
            if r.excl:
                for t in r.r:
                    self._need(eng, waits, t, same_ok=True)
        for w in writes:
            self._need(eng, waits, w.w, same_ok=True)
            for t in w.pw:
                self._need(eng, waits, t, same_ok=True)
            for t in w.r:
                self._need(eng, waits, t, same_ok=True)
        for w in pwrites:
            self._need(eng, waits, w.w, same_ok=True)
            for t in w.r:
                self._need(eng, waits, t, same_ok=True)
        return list(waits.values())

    def _commit(self, tok, reads, writes, pwrites=()):
        for r in reads:
            r.r.append(tok)
        for w in writes:
            w.w = tok
            w.pw = []
            w.r = []
        for w in pwrites:
            w.pw.append(tok)

    def op(self, eng, fn, reads=(), writes=(), signal=True, pwrites=()):
        waits = self._deps(eng, reads, writes, pwrites)
        if signal:
            self.cnt[eng] += 1
            tok = (self.sem[eng], self.cnt[eng], eng)
        else:
            tok = (self.sem[eng], self.cnt[eng] + 1, eng)
        self.ops[eng].append((waits, fn, (self.sem[eng], 1) if signal else None))
        self._commit(tok, reads, writes, pwrites)

    def dma(self, eng, fn, reads=(), writes=(), inc=16):
        waits = self._deps(eng, reads, writes)
        owner = writes[0] if writes else reads[0]
        if owner.dsem is None:
            owner.dsem = self.stack.enter_context(self.nc.semaphore("dsem_%d" % self.nsem))
            self.nsem += 1
        owner.dcnt += (inc if inc is not None else 1)
        tok = (owner.dsem, owner.dcnt, "dma")
        self.ops[eng].append((waits, fn, (owner.dsem, inc)))
        self._commit(tok, reads, writes)

    def alias(self, news, olds):
        for n in news:
            for o_ in olds:
                if o_.w is not None:
                    n.r.append(o_.w)
                n.r.extend(o_.pw)
                n.r.extend(o_.r)

    def finish(self):
        waits = {}
        for r in self.all_res:
            self._need("sp", waits, r.w, same_ok=False)
            for t in r.r + r.pw:
                self._need("sp", waits, t, same_ok=False)
        self.ops["sp"].append((list(waits.values()), None, None))

    def emit(self):
        with self.nc.Block() as block:
            def replay(name):
                def body(e):
                    for waits, fn, sig in self.ops[name]:
                        for sem, val in waits:
                            e.wait_ge(sem, val)
                        if fn is None:
                            continue
                        ins = fn(e)
                        if sig is not None:
                            if sig[1] is None:
                                ins.then_inc(sig[0])
                            else:
                                ins.then_inc(sig[0], sig[1])
                return body
            block.sync(replay("sp"))
            block.tensor(replay("pe"))
            block.scalar(replay("act"))
            block.vector(replay("dve"))
            block.gpsimd(replay("pool"))


PATS = ((1, 0), (4, 1), (16, 2))


def _key_tiles():
    tiles = {}
    info = []
    for d, p in PATS:
        M0 = 1024 // d
        nq = 2048 // d
        for r in range(d):
            for i in range(nq // 128 + 1):
                fs = r + d * (M0 - 64 + 128 * i)
                var = 1 if i == 0 else (2 if i == nq // 128 else 0)
                if d == 16:
                    var = 1 if i == 0 else 2
                tiles[(p, r, i)] = len(info)
                info.append((fs, d, var))
    return tiles, info


class _Stop(Exception):
    pass


def build(dbg=(), stop=None):
    nc = bass.Bass("TRN2", target_bir_lowering=False)

    def din(name, shape):
        return nc.dram_tensor(name, shape, F32, kind="ExternalInput").ap()

    xf_d = din("xf", [4096, 1024])
    win_d = din("w_in", [1024, 2560])
    wout_d = din("w_out", [1024, 1024])
    wup_d = din("w_up", [1024, 4096])
    wdn_d = din("w_down", [4096, 1024])
    pvec_d = din("pvec", [128, 68])
    gw_d = din("gw", [128, 16 * 128])
    tb_d = din("tb", [128, 3 * 8 * 256])
    val_d = din("val", [128, 3 * 192])
    xs_d = din("xs", [3 * 2176, 1024])
    psl_d = din("psl", [128, 96])
    gws_d = din("gws", [128, 24 * 128])
    smask_d = din("smask", [128, 16])
    swp_d = din("swp", [128, 256])
    ident_d = din("ident", [128, 128])
    gfin_d = din("gfin", [128, 1024])
    out_d = nc.dram_tensor("out", [2048, 1024], F32, kind="ExternalOutput").ap()
    dbg_d = {}
    for name, shape in dbg:
        dbg_d[name] = nc.dram_tensor("dbg_" + name, list(shape), F32, kind="ExternalOutput").ap()

    win_v = win_d.rearrange("(kt p) c -> p kt c", p=128)
    wout_v = wout_d.rearrange("(kt p) c -> p kt c", p=128)
    wup_v = wup_d.rearrange("(kt p) c -> p kt c", p=128)
    wdn_v = wdn_d.rearrange("(kt p) c -> p kt c", p=128)
    tb_v = tb_d.rearrange("q (p a c) -> q p a c", p=3, a=4)

    ktiles, kinfo = _key_tiles()
    NKT = len(kinfo)

    with ExitStack() as st:
        S = Sched(nc, st)
        AW = 53200
        arena = st.enter_context(nc.sbuf_tensor("arena", [128, AW], F32))
        banks = [st.enter_context(nc.psum_tensor("bank%d" % i, [128, 512], F32)) for i in range(8)]
        RB = [S.res("bank%d" % i) for i in range(8)]
        for r_ in RB:
            r_.excl = True

        def carve(off, nwords, dt=F32, shape=None, parts=128):
            v = arena[0:parts, off:off + nwords]
            if dt != F32:
                v = v.bitcast(dt)
            if shape is not None:
                if len(shape) == 2:
                    v = v.rearrange("p (a b) -> p a b", a=shape[0])
                elif len(shape) == 3:
                    v = v.rearrange("p (a b c) -> p a b c", a=shape[0], b=shape[1])
            return v

        o = 0
        pv = carve(o, 68); o += 80
        der = carve(o, 40); o += 48
        identb = carve(o, 64, BF16); o += 64
        ones = carve(o, 1); o += 16
        valb = carve(o, 288, BF16, (3, 192)); o += 288
        gwb = carve(o, 1024, BF16, (16, 128)); o += 1024
        cyf = carve(o, 4); o += 16
        cyb = carve(o, 4); o += 16
        ss = carve(o, 32); o += 32
        lnv = carve(o, 32); o += 32
        rs = carve(o, 32); o += 32
        ssq_r = carve(o, 16); o += 16
        ssq_a = carve(o, 16); o += 16
        rstd_r = carve(o, 16); o += 16
        rstd_a = carve(o, 16); o += 16
        swp = carve(o, 256, F32, (2, 128)); o += 256
        psl = carve(o, 96); o += 96
        ders = carve(o, 48); o += 48
        smask = carve(o, 16); o += 16
        ends = carve(o, 12); o += 16
        inits = carve(o, 12); o += 16
        ssx = carve(o, 64); o += 64
        lnx = carve(o, 64); o += 64
        rsx = carve(o, 64); o += 64
        CONST_END = 2368
        assert o <= CONST_END
        H0 = CONST_END
        M0_ = H0 + 16384
        L0 = M0_ + 8192
        LW = AW - L0
        hT = carve(H0, 16384, BF16, (8, 4096))
        mixT = carve(M0_, 8192, BF16, (8, 2048))

        Rpv, Rder, Rid, Rones, Rval, Rgw = (S.res(n) for n in ("pv", "der", "ident", "ones", "val", "gw"))
        Rcy, Rpsl, Rders, Rsm, Rends, Rinits, Rgws = (S.res(n) for n in ("cy", "psl", "ders", "smask", "ends", "inits", "gws"))
        Rss, Rssq_r, Rssq_a, Rrstd = (S.res(n) for n in ("ss", "ssq_r", "ssq_a", "rstd"))
        Rswp = S.res("swp")
        Rh = [S.res("hT%d" % i) for i in range(8)]
        RmixR = [S.res("mixR%d" % i) for i in range(4)]
        RmixA = [S.res("mixA%d" % i) for i in range(4)]
        Rout = S.res("out")
        Rdbg = S.res("dbg")

        def dma(eng, out, in_, reads, writes):
            S.dma(eng, lambda e: e.dma_start(out=out, in_=in_), reads=reads, writes=writes)

        def dump(name, ap, res):
            if name in dbg_d:
                dma("pool", dbg_d[name], ap, list(res), [Rdbg])

        def act(out, in_, func, reads, writes, pw=(), **kw):
            S.op("act", lambda e: e.activation(out=out, in_=in_, func=func, **kw), reads=reads, writes=writes, pwrites=pw)

        def mm(out, lhsT, rhs, start, stop, reads, writes, signal):
            S.op("pe", lambda e: e.matmul(out, lhsT=lhsT, rhs=rhs, start=start, stop=stop),
                 reads=reads, writes=writes, signal=signal)

        def tt(eng, out, in0, in1, op, reads, writes):
            S.op(eng, lambda e: e.tensor_tensor(out=out, in0=in0, in1=in1, op=op), reads=reads, writes=writes)

        def ts(eng, out, in0, s1, s2, op0, op1, reads, writes, pw=()):
            if s2 is None:
                S.op(eng, lambda e: e.tensor_scalar(out=out, in0=in0, scalar1=s1, scalar2=None, op0=op0),
                     reads=reads, writes=writes, pwrites=pw)
            else:
                S.op(eng, lambda e: e.tensor_scalar(out=out, in0=in0, scalar1=s1, scalar2=s2, op0=op0, op1=op1),
                     reads=reads, writes=writes, pwrites=pw)

        def stt(out, in0, scalar, in1, op0, op1, reads, writes):
            S.op("dve", lambda e: e.scalar_tensor_tensor(out=out, in0=in0, scalar=scalar, in1=in1, op0=op0, op1=op1),
                 reads=reads, writes=writes)

        def ckpt(name):
            if stop == name:
                raise _Stop()

        def hres(f0, f1):
            return [Rh[b] for b in range(f0 // 512, (f1 - 1) // 512 + 1)]

        try:
            dma("sp", pv, pvec_d, [], [Rpv])
            dma("sp", psl, psl_d, [], [Rpsl])
            dma("sp", smask, smask_d, [], [Rsm])
            dma("sp", swp, swp_d.rearrange("p (a b) -> p a b", a=2), [], [Rswp])
            dma("pool", identb, ident_d, [], [Rid])
            dma("pool", gwb, gw_d.rearrange("p (m c) -> p m c", m=16), [], [Rgw])
            dma("pool", valb, val_d.rearrange("p (m c) -> p m c", m=3), [], [Rval])
            S.op("dve", lambda e: e.memset(ones, 1.0), writes=[Rones])
            act(der[:, 0:4], pv[:, 36:40], AF.Exp, [Rpv], [Rder], scale=-1.0)
            act(der[:, 4:8], pv[:, 48:52], AF.Exp, [Rpv], [Rder], scale=-1.0)
            act(der[:, 0:8], der[:, 0:8], AF.Ln, [Rder], [Rder], bias=1.0)
            ts("dve", der[:, 0:8], der[:, 0:8], -8.0, None, ALU.mult, None, [Rder], [Rder])
            ts("dve", der[:, 8:16], der[:, 0:8], 0.5, None, ALU.mult, None, [Rder], [Rder])
            ts("dve", der[:, 16:24], der[:, 0:8], 1024.0, None, ALU.mult, None, [Rder], [Rder])
            ts("dve", der[:, 24:32], pv[:, 28:36], 0.5, None, ALU.mult, None, [Rpv, Rder], [Rder])
            ts("dve", der[:, 32:40], pv[:, 40:48], 0.5, None, ALU.mult, None, [Rpv, Rder], [Rder])

            def norm_stream(tiles):
                def stage_a(t):
                    if t["load"] is not None:
                        t["load"]()
                    ss_t, lnv_t, rs_t = t["st"]
                    i = t["idx"]
                    Rst = t["Rst"] = S.res("st")
                    act(t["xnb"], t["src"], AF.Square, [t["Rsrc"]], [t["Rxnb"], Rst], accum_out=ss_t[:, i:i + 1])
                    act(lnv_t[:, i:i + 1], ss_t[:, i:i + 1], AF.Ln, [Rst], [Rst], scale=1.0 / 1024, bias=EPS)
                    act(rs_t[:, i:i + 1], lnv_t[:, i:i + 1], AF.Exp, [Rst], [Rst], scale=-0.5)
                    ts("dve", t["xnb"], t["src"], rs_t[:, i:i + 1], None, ALU.mult, None, [t["Rsrc"], Rst], [t["Rxnb"]])

                def stage_b(t):
                    pb = t["pbank"][:].bitcast(BF16).rearrange("p (a b) -> p a b", a=8)
                    xnb_ = t["xnb"]
                    for kt in range(8):
                        S.op("pe", lambda e, kt=kt, pb=pb, xnb_=xnb_: e.transpose(out=pb[:, kt, :],
                                                                               in_=xnb_[:, kt * 128:(kt + 1) * 128],
                                                                               identity=identb),
                             reads=[t["Rxnb"], Rid], writes=[t["Rpb"]], signal=(kt == 7))
                    g0 = t["gcol0"]
                    gb = pv[:, g0:g0 + 8].unsqueeze(2).to_broadcast([128, 8, 128])
                    dst_ = t["dst"][:, :, t["dstcols"]]
                    S.op("dve", lambda e, dst_=dst_, pb=pb, gb=gb: e.tensor_tensor(out=dst_, in0=pb, in1=gb, op=ALU.mult),
                         reads=[t["Rpb"], Rpv], pwrites=t["Rdst"])

                for n in range(len(tiles) + 1):
                    if n < len(tiles):
                        stage_a(tiles[n])
                    if n >= 1:
                        stage_b(tiles[n - 1])

            xt = [carve(L0 + i * 1024, 1024) for i in range(3)]
            Rxt = [S.res("xt%d" % i) for i in range(3)]
            sqj = carve(L0 + 3072, 512, BF16)
            Rsqj = S.res("sqj")
            xnb = [carve(L0 + 3584 + i * 512, 512, BF16) for i in range(2)]
            Rxnb = [S.res("xnb%d" % i) for i in range(2)]
            order = list(range(8, 24)) + list(range(0, 8)) + list(range(24, 32))
            tiles = []
            for n, tti in enumerate(order):
                xb, Rx = xt[n % 3], Rxt[n % 3]
                tiles.append(dict(load=(lambda xb=xb, Rx=Rx, tti=tti: dma("sp", xb, xf_d[tti * 128:(tti + 1) * 128, :], [], [Rx])),
                                  src=xb, Rsrc=Rx, idx=tti, st=(ss, lnv, rs), xnb=xnb[n % 2], Rxnb=Rxnb[n % 2],
                                  pbank=banks[n % 2], Rpb=RB[n % 2], gcol0=0, dst=hT,
                                  dstcols=slice(tti * 128, (tti + 1) * 128), Rdst=[Rh[tti // 4]]))
            norm_stream(tiles)
            dump("hT0", hT[:, 0, :], Rh)
            ckpt("p1")

            for s_ in range(3):
                b0 = s_ * 32
                d0 = s_ * 16
                act(ders[:, d0 + 12:d0 + 16], psl[:, b0 + 28:b0 + 32], AF.Exp, [Rpsl], [Rders], scale=-1.0)
                act(ders[:, d0 + 12:d0 + 16], ders[:, d0 + 12:d0 + 16], AF.Ln, [Rders], [Rders], bias=1.0)
                ts("dve", ders[:, d0:d0 + 4], ders[:, d0 + 12:d0 + 16], -4.0, None, ALU.mult, None, [Rders], [Rders])
                ts("dve", ders[:, d0 + 4:d0 + 12], psl[:, b0 + 20:b0 + 28], 0.5, None, ALU.mult, None, [Rpsl, Rders], [Rders])
                ts("dve", ders[:, d0 + 12:d0 + 16], ders[:, d0 + 12:d0 + 16], -8.0, None, ALU.mult, None, [Rders], [Rders])
            gws = carve(M0_, 1536, BF16, (24, 128))
            dma("pool", gws, gws_d.rearrange("p (m c) -> p m c", m=24), [], [Rgws])
            wrg = carve(M0_ + 4096, 4096, BF16, (8, 1024))
            Rwrg = S.res("wrg")
            for kt in range(8):
                dma("pool", wrg[:, kt, :], win_v[:, kt, 0:1024], [], [Rwrg])
            lo = L0 + 4608
            xr = carve(lo, 2064); lo += 2064
            xc2 = [carve(lo, 2048), carve(lo + 2048, 2048)]; lo += 4096
            rta = carve(lo, 2048); lo += 2048
            itb = carve(lo, 2048); lo += 2048
            sqs = carve(lo, 2048); lo += 2048
            hTs = carve(lo, 8704, BF16, (8, 2176))
            gg = carve(lo, 2048)
            ysq = carve(lo + 2048, 2048)
            hl = [carve(lo + 4096, 2048), carve(lo + 6144, 2048)]
            lo += 8704
            assert lo <= AW, lo
            xcb2 = [carve(M0_ + 2048, 1024, BF16), carve(M0_ + 3072, 1024, BF16)]
            Rxrp = [S.res("xr%d" % i) for i in range(6)]
            Rxc2 = [[S.res("xc%d_%d" % (i, t)) for t in range(4)] for i in range(2)]
            Rxcb2 = [[S.res("xcb%d_%d" % (i, t)) for t in range(4)] for i in range(2)]
            Rrta = [S.res("rta%d" % t) for t in range(4)]
            Ritb = [S.res("itb%d" % t) for t in range(4)]
            Rsqs = [S.res("sqs%d" % t) for t in range(4)]
            RhTs, Rgg, Rysq = S.res("hTs"), S.res("gg"), S.res("ysq")
            Rhl = [S.res("hl0"), S.res("hl1")]
            B4 = [slice(t * 512, (t + 1) * 512) for t in range(4)]

            def conv_only(ct, buf, cwcol, cbcol, src_pv, Rsrc_pv):
                xc = xc2[buf]
                rx = [[Rxrp[t], Rxrp[t + 1], Rxrp[t + 2]] for t in range(4)]
                for t in range(4):
                    c0 = t * 512
                    ts("dve", xc[:, B4[t]], xr[:, c0:c0 + 512], src_pv[:, cwcol + ct * 4:cwcol + ct * 4 + 1],
                       src_pv[:, cbcol + ct:cbcol + ct + 1], ALU.mult, ALU.add, rx[t] + [Rsrc_pv], [Rxc2[buf][t]])
                for k in range(1, 4):
                    for t in range(4):
                        c0 = t * 512
                        stt(xc[:, B4[t]], xr[:, c0 + k:c0 + k + 512], src_pv[:, cwcol + ct * 4 + k:cwcol + ct * 4 + k + 1],
                            xc[:, B4[t]], ALU.mult, ALU.add, rx[t] + [Rsrc_pv, Rxc2[buf][t]], [Rxc2[buf][t]])

            def cast_only(buf):
                xc, xcb = xc2[buf], xcb2[buf]
                for t in range(4):
                    S.op("act", lambda e, blk=B4[t], xc=xc, xcb=xcb: e.copy(out=xcb[:, blk], in_=xc[:, blk]),
                         reads=[Rxc2[buf][t]], writes=[Rxcb2[buf][t]])

            def gate_pieces(buf, gA, gX, Rg, hbA, hbX, hc, c2, Rp, init_fn, reverse, hout, Rhout):
                xc, xcb = xc2[buf], xcb2[buf]

                def g1():
                    for t in range(4):
                        blk = B4[t]
                        ba_, bx_ = 4 + (t % 2) * 2, 5 + (t % 2) * 2
                        mm(banks[ba_][:, 0:512], gA, xcb[:, blk], True, True, [Rg, Rxcb2[buf][t]], [RB[ba_]], True)
                        mm(banks[bx_][:, 0:512], gX, xcb[:, blk], True, True, [Rg, Rxcb2[buf][t]], [RB[bx_]], True)
                        act(rta[:, blk], banks[ba_][:, 0:512], AF.Tanh, [RB[ba_], Rp], [Rrta[t]], scale=0.5, bias=hbA)
                        act(itb[:, blk], banks[bx_][:, 0:512], AF.Tanh, [RB[bx_], Rp], [Ritb[t]], scale=0.5, bias=hbX)

                def g2():
                    for t in range(4):
                        act(sqs[:, B4[t]], rta[:, B4[t]], AF.Exp, [Rrta[t], Rp], [Rsqs[t]], scale=c2, bias=c2)
                        act(rta[:, B4[t]], rta[:, B4[t]], AF.Exp, [Rrta[t], Rp], [Rrta[t]], scale=hc, bias=hc)

                def g3():
                    for t in range(4):
                        blk = B4[t]
                        stt(itb[:, blk], itb[:, blk], 1.0, xc[:, blk], ALU.add, ALU.mult, [Ritb[t], Rxc2[buf][t]], [Ritb[t]])

                def g4():
                    for t in range(4):
                        blk = B4[t]
                        act(sqs[:, blk], sqs[:, blk], AF.Sqrt, [Rsqs[t]], [Rsqs[t]], scale=-0.25, bias=0.25)
                        tt("pool", itb[:, blk], itb[:, blk], sqs[:, blk], ALU.mult, [Ritb[t], Rsqs[t]], [Ritb[t]])

                def g5():
                    init, Rinit = init_fn()
                    order_ = range(3, -1, -1) if reverse else range(4)
                    prev = None
                    for t in order_:
                        blk = B4[t]
                        if prev is None:
                            ini, Ri = init, Rinit
                        elif reverse:
                            ini, Ri = hout[:, 512 * (t + 1):512 * (t + 1) + 1], Rhout
                        else:
                            ini, Ri = hout[:, 512 * t - 1:512 * t], Rhout
                        if reverse:
                            S.op("dve", lambda e, blk=blk, ini=ini: e.tensor_tensor_scan(
                                out=hout[:, blk][:, ::-1], data0=rta[:, blk][:, ::-1], data1=itb[:, blk][:, ::-1],
                                initial=ini, op0=ALU.mult, op1=ALU.add),
                                 reads=[Rrta[t], Ritb[t]] + Ri, writes=[Rhout[t]] if len(Rhout) == 4 else Rhout)
                        else:
                            S.op("dve", lambda e, blk=blk, ini=ini: e.tensor_tensor_scan(
                                out=hout[:, blk], data0=rta[:, blk], data1=itb[:, blk],
                                initial=ini, op0=ALU.mult, op1=ALU.add),
                                 reads=[Rrta[t], Ritb[t]] + Ri, writes=[Rhout[t]] if len(Rhout) == 4 else Rhout)
                        prev = t
                return g1, g2, g3, g4, g5

            jobn = [0]

            def slot_job(s_, ct):
                buf = jobn[0] % 2
                jobn[0] += 1
                wx = slice(ct * 128, (ct + 1) * 128)
                d0 = s_ * 16

                def f1():
                    for bi, (fc0, n_) in enumerate(((62, 512), (574, 512), (1086, 512), (1598, 512), (2110, 4))):
                        bk = bi % 2
                        for kt in range(8):
                            mm(banks[bk][:, 0:n_], wrg[:, kt, wx], hTs[:, kt, fc0:fc0 + n_], kt == 0, kt == 7,
                               [Rwrg, RhTs], [RB[bk]], kt == 7)
                        act(xr[:, fc0 - 62:fc0 - 62 + n_], banks[bk][:, 0:n_], AF.Copy, [RB[bk]], [Rxrp[bi + 1]])

                def f2():
                    conv_only(ct, buf, s_ * 32, s_ * 32 + 16, psl, Rpsl)

                def f3():
                    cast_only(buf)

                def init_fn():
                    if s_ == 0:
                        return 0.0, []
                    ts("dve", inits[:, s_ * 4 + ct:s_ * 4 + ct + 1], ends[:, (s_ - 1) * 4 + ct:(s_ - 1) * 4 + ct + 1],
                       smask[:, s_:s_ + 1], None, ALU.mult, None, [Rends, Rsm], [Rinits])
                    return inits[:, s_ * 4 + ct:s_ * 4 + ct + 1], [Rinits]

                g = gate_pieces(buf, gws[:, s_ * 8 + ct, :], gws[:, s_ * 8 + 4 + ct, :], Rgws,
                                ders[:, d0 + 4 + ct:d0 + 5 + ct], ders[:, d0 + 8 + ct:d0 + 9 + ct], ders[:, d0 + ct:d0 + ct + 1],
                                ders[:, d0 + 12 + ct:d0 + 13 + ct], Rders, init_fn, False, sqs, Rsqs)

                def fin():
                    S.op("dve", lambda e: e.tensor_copy(out=ends[:, s_ * 4 + ct:s_ * 4 + ct + 1], in_=sqs[:, 2047:2048]),
                         reads=[Rsqs[3]], writes=[Rends])
                return dict(f=(f1, f2, f3), g=[g], fin=fin, pre=None)

            def own_job(ct):
                buf = jobn[0] % 2
                jobn[0] += 1
                wx = slice(ct * 128, (ct + 1) * 128)
                wg = slice(512 + ct * 128, 512 + (ct + 1) * 128)

                def f1():
                    for t in range(4):
                        bk = t % 2
                        f0 = 1024 + 512 * t
                        for kt in range(8):
                            mm(banks[bk][:, 0:512], wrg[:, kt, wx], hT[:, kt, f0:f0 + 512], kt == 0, kt == 7,
                               [Rwrg] + hres(f0, f0 + 512), [RB[bk]], kt == 7)
                        act(xr[:, 2 + 512 * t:2 + 512 * (t + 1)], banks[bk][:, 0:512], AF.Copy, [RB[bk]], [Rxrp[1 + t]])
                    for kt in range(8):
                        mm(banks[2][:, 0:2], wrg[:, kt, wx], hT[:, kt, 1022:1024], kt == 0, kt == 7, [Rwrg, Rh[1]], [RB[2]], False)
                    for kt in range(8):
                        mm(banks[2][:, 2:4], wrg[:, kt, wx], hT[:, kt, 3072:3074], kt == 0, kt == 7, [Rwrg, Rh[6]], [RB[2]], kt == 7)
                    act(xr[:, 0:2], banks[2][:, 0:2], AF.Copy, [RB[2]], [Rxrp[0]])
                    act(xr[:, 2050:2052], banks[2][:, 2:4], AF.Copy, [RB[2]], [Rxrp[5]])

                def f2():
                    conv_only(ct, buf, 8, 24, pv, Rpv)

                def f3():
                    cast_only(buf)

                def pre():
                    if ct == 0:
                        carries()
                    for t in range(4):
                        bk = 2 + t % 2
                        f0 = 1024 + 512 * t
                        for kt in range(8):
                            mm(banks[bk][:, 0:512], wrg[:, kt, wg], hT[:, kt, f0:f0 + 512], kt == 0, kt == 7,
                               [Rwrg] + hres(f0, f0 + 512), [RB[bk]], kt == 7)
                        act(gg[:, B4[t]], banks[bk][:, 0:512], AF.Gelu_apprx_tanh, [RB[bk]], [Rgg])
                    if ct == 0:
                        dump("xc0", xc2[buf], Rxc2[buf])
                        dump("gg0", gg, [Rgg])

                gs = []
                for d in range(2):
                    cy = cyf if d == 0 else cyb
                    gs.append(gate_pieces(buf, gwb[:, (2 * d) * 4 + ct, :], gwb[:, (2 * d + 1) * 4 + ct, :], Rgw,
                                          der[:, 24 + 8 * d + ct:25 + 8 * d + ct], der[:, 28 + 8 * d + ct:29 + 8 * d + ct],
                                          der[:, 8 + 4 * d + ct:9 + 4 * d + ct], der[:, 4 * d + ct:4 * d + ct + 1], Rder,
                                          (lambda cy=cy: (cy[:, ct:ct + 1], [Rcy])), d == 1, hl[d], [Rhl[d]]))

                def fin():
                    y = hl[0]
                    tt("dve", y, hl[0], hl[1], ALU.add, [Rhl[0], Rhl[1]], [Rhl[0]])
                    tt("pool", y, y, gg, ALU.mult, [Rhl[0], Rgg], [Rhl[0]])
                    if ct == 0:
                        dump("y0", y, [Rhl[0]])
                    act(ysq, y, AF.Square, [Rhl[0]], [Rysq])
                    ts("dve", mixT[:, ct, :], y, pv[:, 52 + ct:53 + ct], None, ALU.mult, None, [Rhl[0], Rpv],
                       [RmixR[ct]] + (Rxcb2[buf] if ct >= 2 else []))

                def fin_stats():
                    for i in range(16):
                        mm(banks[7][:, i:i + 1], ysq[:, 128 * i:128 * (i + 1)], ones, True, True, [Rysq, Rones], [RB[7]], i == 15)

                def fin_ssq():
                    if ct == 0:
                        S.op("dve", lambda e: e.tensor_copy(out=ssq_r, in_=banks[7][:, 0:16]), reads=[RB[7]], writes=[Rssq_r])
                    else:
                        tt("dve", ssq_r, banks[7][:, 0:16], ssq_r, ALU.add, [RB[7], Rssq_r], [Rssq_r])
                return dict(f=(f1, f2, f3), g=gs, fin=fin, pre=pre, fin_stats=fin_stats, fin_ssq=fin_ssq)

            def slot_norm(s_):
                tiles = []
                for tti in range(17):
                    n = s_ * 17 + tti
                    xb, Rx = xt[n % 3], Rxt[n % 3]
                    r0 = s_ * 2176 + tti * 128
                    tiles.append(dict(load=(lambda xb=xb, Rx=Rx, r0=r0: dma("sp", xb, xs_d[r0:r0 + 128, :], [], [Rx])),
                                      src=xb, Rsrc=Rx, idx=n, st=(ssx, lnx, rsx), xnb=xnb[n % 2], Rxnb=Rxnb[n % 2],
                                      pbank=banks[2 + n % 2], Rpb=RB[2 + n % 2], gcol0=0, dst=hTs,
                                      dstcols=slice(tti * 128, (tti + 1) * 128), Rdst=[RhTs]))
                norm_stream(tiles)

            def carries():
                dump("ends", ends, [Rends])
                S.op("dve", lambda e: e.memset(cyf, 0.0), writes=[Rcy])
                S.op("dve", lambda e: e.memset(cyb, 0.0), writes=[Rcy])
                for s_ in range(3):
                    stt(cyf, ends[:, s_ * 4:s_ * 4 + 4], smask[:, 3 + s_:4 + s_], cyf, ALU.mult, ALU.add, [Rends, Rsm, Rcy], [Rcy])
                    stt(cyb, ends[:, s_ * 4:s_ * 4 + 4], smask[:, 6 + s_:7 + s_], cyb, ALU.mult, ALU.add, [Rends, Rsm, Rcy], [Rcy])

            specs = [("slot", s_, ct) for s_ in range(3) for ct in range(4)] + [("own", None, ct) for ct in range(4)]

            def make(spec):
                kind, s_, ct = spec
                if kind == "slot":
                    if ct == 0:
                        slot_norm(s_)
                    return slot_job(s_, ct)
                if ct == 0:
                    S.alias([Rgg, Rysq] + Rhl, [RhTs])
                    S.alias(RmixR, [Rgws])
                if ct == 2:
                    early_pair0()
                return own_job(ct)

            wq = [carve(L0 + i * 1536, 1536, BF16, (8, 384)) for i in range(2)]
            tbp = [carve(L0 + 3072 + i * 1536, 1536, F32, (3, 512)) for i in range(2)]
            Rwq = [S.res("wq0"), S.res("wq1")]
            Rtbp = [S.res("tbp0"), S.res("tbp1")]

            def load_pair(hp):
                b = hp % 2
                for i, c0 in enumerate((1024, 1536, 2048)):
                    dma("pool", wq[b][:, :, i * 128:(i + 1) * 128], win_v[:, :, c0 + hp * 128:c0 + (hp + 1) * 128],
                        [], [Rwq[b]])
                dma("sp", tbp[b], tb_v[:, :, hp, :], [], [Rtbp[b]])

            def early_pair0():
                S.alias(Rwq + [Rtbp[0]], Rxt + Rxnb + [Rsqj])
                load_pair(0)

            jobs = {}

            def job(i):
                if i >= len(specs):
                    return None
                if i not in jobs:
                    jobs[i] = make(specs[i])
                return jobs[i]

            pend_stats, pend_ssq = [None], [None]
            j0 = job(0)
            for f in j0["f"]:
                f()
            if job(1) is not None:
                job(1)["f"][0]()
            for i in range(len(specs)):
                cur, n1, n2 = job(i), job(i + 1), job(i + 2)
                if cur["pre"] is not None:
                    cur["pre"]()
                gl = cur["g"]
                g1, g2, g3, g4, g5 = gl[0]
                if n1 is not None:
                    n1["f"][1]()
                g1()
                if pend_stats[0] is not None:
                    pend_stats[0]()
                    pend_stats[0] = None
                g2()
                g3()
                if pend_ssq[0] is not None:
                    pend_ssq[0]()
                    pend_ssq[0] = None
                if n1 is not None:
                    n1["f"][2]()
                g4()
                if n2 is not None:
                    n2["f"][0]()
                g5()
                if len(gl) == 2:
                    h1, h2, h3, h4, h5 = gl[1]
                    h1(); h2(); h3(); h4(); h5()
                cur["fin"]()
                if "fin_stats" in cur:
                    if n1 is not None:
                        pend_stats[0], pend_ssq[0] = cur["fin_stats"], cur["fin_ssq"]
                    else:
                        cur["fin_stats"]()
                        cur["fin_ssq"]()
            ckpt("rnn")
            lo = L0
            lo += 6144
            Qz = carve(lo, 2048, BF16, (2, 2048)); lo += 2048
            Kp = carve(lo, 2048, BF16); lo += 2048
            Vp = carve(lo, NKT * 96, BF16, (NKT, 192)); lo += NKT * 96
            acc = carve(lo, 4096, F32, (2, 2048)); lo += 4096
            NU = 3
            sS = [carve(lo + i * 512, 512, F32) for i in range(2)]; lo += 1024
            PT = [carve(lo + i * 256, 256, BF16) for i in range(2)]; lo += 512
            ftmp = [carve(lo + i * 512, 512) for i in range(2)]; lo += 1024
            VT = carve(lo, 2048, BF16); lo += 2048
            RVT = S.res("VT")
            otmp = [carve(lo + i * 256, 256, F32, (2, 128)) for i in range(2)]; lo += 512
            Rotmp = [S.res("otmp0"), S.res("otmp1")]
            assert lo <= AW, lo
            RQ, RK, RV, Racc = S.res("Qp"), S.res("Kp"), S.res("Vp"), S.res("accL")
            RaccH = S.res("accH")
            RsS = [S.res("sS%d" % i) for i in range(2)]
            RsSk = [[S.res("sS%d_%d" % (i, k)) for k in range(2)] for i in range(2)]
            RPT = [S.res("PT%d" % i) for i in range(2)]
            Rft = [S.res("ft%d" % i) for i in range(2)]
            old_locals = (Rxrp + Rxc2[0] + Rxc2[1] + [Rgg] + Rrta + Ritb + Rsqs + [RhTs, Rysq, Rsqj] + Rhl + Rxt + Rxnb)
            S.alias(Rwq + Rtbp + [RQ, RK, RV, Racc, RaccH, RVT] + RsS + RsSk[0] + RsSk[1] + RPT + Rft + Rotmp, old_locals)

            units = []
            for d, p in PATS:
                for r in range(d):
                    for qi in range(2048 // d // 128):
                        units.append((p, d, r, qi))
            Vv = Vp.rearrange("p t (b c) -> p t b c", b=3)
            for ti_ in range(NKT):
                var_ = kinfo[ti_][2]
                S.op("pool", lambda e, ti_=ti_, var_=var_: e.tensor_copy(out=Vp[:, ti_, 64:128], in_=valb[:, var_, 0:64]),
                     reads=[Rval], pwrites=[RV])
            pbank = [0]

            def nb():
                bk = pbank[0] % 6
                pbank[0] += 1
                return bk

            def proj_groups(hp):
                wb = hp % 2
                groups = []

                def gq(tbk):
                    bk = nb()
                    f0 = 1024 + 512 * tbk
                    for kt in range(8):
                        mm(banks[bk][:, 0:512], wq[wb][:, kt, 0:128], hT[:, kt, f0:f0 + 512], kt == 0, kt == 7,
                           [Rwq[wb]] + hres(f0, f0 + 512), [RB[bk]], kt == 7)
                    S.op("act", lambda e: e.copy(out=Qz[0:64, 0, 512 * tbk:512 * (tbk + 1)], in_=banks[bk][0:64, 0:512]),
                         reads=[RB[bk]], pwrites=[RQ])
                    S.op("act", lambda e: e.copy(out=Qz[64:128, 1, 512 * tbk:512 * (tbk + 1)], in_=banks[bk][64:128, 0:512]),
                         reads=[RB[bk]], pwrites=[RQ])

                def gk(fb):
                    bk = nb()
                    f0 = 512 * fb
                    for kt in range(8):
                        mm(banks[bk][:, 0:512], wq[wb][:, kt, 128:256], hT[:, kt, f0:f0 + 512], kt == 0, kt == 7,
                           [Rwq[wb], Rh[fb]], [RB[bk]], kt == 7)
                    S.op("dve", lambda e: e.tensor_copy(out=Kp[:, 512 * fb:512 * (fb + 1)], in_=banks[bk][:, 0:512]),
                         reads=[RB[bk]], pwrites=[RK])

                def gv(fb):
                    bk = nb()
                    f0 = 512 * fb
                    for kt in range(8):
                        mm(banks[bk][:, 0:512], wq[wb][:, kt, 256:384], hT[:, kt, f0:f0 + 512], kt == 0, kt == 7,
                           [Rwq[wb], Rh[fb]], [RB[bk]], kt == 7)
                    S.op("act", lambda e: e.copy(out=VT[:, 512 * fb:512 * (fb + 1)], in_=banks[bk][:, 0:512]),
                         reads=[RB[bk]], pwrites=[RVT])

                def gt(t0):
                    bk = nb()
                    nt = min(8, NKT - t0)
                    pbv = banks[bk][:].bitcast(BF16).rearrange("p (a b) -> p a b", a=8)
                    for ti in range(nt):
                        fs, d, _ = kinfo[t0 + ti]
                        S.op("pe", lambda e, ti=ti, fs=fs, d=d: e.transpose(out=pbv[:, ti, :], in_=VT[:, fs:fs + 127 * d + 1:d],
                                                                          identity=identb),
                             reads=[RVT, Rid], writes=[RB[bk]], signal=(ti == nt - 1))
                    src = pbv[:, 0:nt, :].rearrange("p t (b c) -> p t b c", b=2)
                    if (t0 // 8) % 2 == 0:
                        S.op("act", lambda e: e.copy(out=Vv[:, t0:t0 + nt, 0:3:2, :], in_=src), reads=[RB[bk]], pwrites=[RV])
                    else:
                        S.op("dve", lambda e: e.tensor_copy(out=Vv[:, t0:t0 + nt, 0:3:2, :], in_=src),
                             reads=[RB[bk]], pwrites=[RV])

                for tbk in range(4):
                    groups.append(lambda tbk=tbk: gq(tbk))
                for fb in range(8):
                    groups.append(lambda fb=fb: gk(fb))
                for fb in range(8):
                    groups.append(lambda fb=fb: gv(fb))
                for t0 in range(0, NKT, 8):
                    groups.append(lambda t0=t0: gt(t0))
                return groups

            def run_units(hp):
                wb = hp % 2

                def qk(u, un):
                    p, d, r, qi = un
                    sb = (u % NU) * 2
                    t0i, t1i = ktiles[(p, r, qi)], ktiles[(p, r, qi + 1)]
                    q0 = r + 128 * d * qi
                    for kk, ti in enumerate((t0i, t1i)):
                        fs = kinfo[ti][0]
                        mm(banks[sb][:, kk * 256:(kk + 1) * 256], Kp[:, fs:fs + 127 * d + 1:d],
                           Qz[:, :, q0:q0 + 127 * d + 1:d], True, True, [RK, RQ], [RB[sb]], kk == 1)
                    for kk in range(2):
                        cs = slice(kk * 256, (kk + 1) * 256)
                        S.op("dve", lambda e, cs=cs: e.scalar_tensor_tensor(out=sS[u % 2][:, cs], in0=banks[sb][:, cs], scalar=0.125,
                                                                           in1=tbp[wb][:, p, cs], op0=ALU.mult, op1=ALU.add),
                             reads=[RB[sb], Rtbp[wb]], pwrites=[RsSk[u % 2][kk]])
                        S.op("act", lambda e, cs=cs: e.activation(out=PT[u % 2][:, cs], in_=sS[u % 2][:, cs], func=AF.Exp),
                             reads=[RsSk[u % 2][kk]], pwrites=[RPT[u % 2]])

                def pv_(u, un):
                    p, d, r, qi = un
                    ob = (u % NU) * 2 + 1
                    t0i, t1i = ktiles[(p, r, qi)], ktiles[(p, r, qi + 1)]
                    q0 = r + 128 * d * qi
                    for hh in range(2):
                        for kk, ti in enumerate((t0i, t1i)):
                            c0 = kk * 256 + hh * 128
                            mm(banks[ob][:, 128 * hh:128 * (hh + 1)], Vp[:, ti, 64 * hh:64 * hh + 128],
                               PT[u % 2][:, c0:c0 + 128], kk == 0, kk == 1, [RV, RPT[u % 2]], [RB[ob]],
                               (hh == 1 and kk == 1))
                    src = banks[ob][:, 0:256].rearrange("p (a b) -> p a b", a=2)
                    dst = acc[:, :, q0:q0 + 127 * d + 1:d]
                    if p == 0:
                        S.op("act", lambda e: e.copy(out=dst, in_=src), reads=[RB[ob]], writes=[Racc])
                    else:
                        tmp_, Rt_ = otmp[u % 2], Rotmp[u % 2]
                        S.op("act", lambda e: e.copy(out=tmp_, in_=src), reads=[RB[ob]], writes=[Rt_])
                        tt("pool", dst, tmp_, dst, ALU.add, [Rt_, Racc], [Racc])

                for u, un in enumerate(units):
                    qk(u, un)
                    if u >= 1:
                        pv_(u - 1, units[u - 1])
                pv_(len(units) - 1, units[-1])

            def fin_parts(hp):
                parts = []
                f_rec, f_o = ftmp[0], ftmp[1]
                Rrec, Ro = Rft[0], Rft[1]

                def p1(tbk):
                    blk = slice(512 * tbk, 512 * (tbk + 1))
                    mm(banks[7][:, 0:512], swp[:, 0, :], acc[:, 0, blk], True, False, [Rswp, Racc], [RB[7]], False)
                    mm(banks[7][:, 0:512], swp[:, 1, :], acc[:, 1, blk], False, True, [Rswp, Racc], [RB[7]], True)
                    act(f_rec, banks[7][:, 0:512], AF.Ln, [RB[7]], [Rrec])
                    act(f_rec, f_rec, AF.Exp, [Rrec], [Rrec], scale=-1.0)
                    tt("dve", f_o[0:64, :], acc[0:64, 0, blk], f_rec[0:64, :], ALU.mult, [Racc, Rrec], [Ro])
                    tt("dve", f_o[64:128, :], acc[64:128, 1, blk], f_rec[64:128, :], ALU.mult, [Racc, Rrec, Ro], [Ro])
                    ts("dve", mixT[:, 4 + hp, blk], f_o, pv[:, 56 + hp:57 + hp], None, ALU.mult, None, [Ro, Rpv], [RmixA[hp]])
                    act(f_rec, f_o, AF.Square, [Ro], [Rrec])

                def p2(tbk):
                    for i in range(4):
                        col = 4 * tbk + i
                        mm(banks[6][:, col:col + 1], f_rec[:, 128 * i:128 * (i + 1)], ones, True, True, [Rrec, Rones], [RB[6]],
                           i == 3)

                def pend():
                    if hp == 0:
                        S.op("dve", lambda e: e.tensor_copy(out=ssq_a, in_=banks[6][:, 0:16]), reads=[RB[6]], writes=[Rssq_a])
                    else:
                        tt("dve", ssq_a, banks[6][:, 0:16], ssq_a, ALU.add, [RB[6], Rssq_a], [Rssq_a])
                for tbk in range(4):
                    parts.append((lambda tbk=tbk: p1(tbk), lambda tbk=tbk: p2(tbk)))
                return parts, pend

            S.op("pool", lambda e: e.memset(Qz, 0.0), writes=[RQ])
            load_pair(1)
            for g_ in proj_groups(0):
                g_()
            for hp in range(4):
                if 1 <= hp < 3:
                    load_pair(hp + 1)
                run_units(hp)
                parts, pend = fin_parts(hp)
                nxt = proj_groups(hp + 1) if hp + 1 < 4 else []
                per = (len(nxt) + 3) // 4
                for b_ in range(4):
                    parts[b_][0]()
                    for g_ in nxt[b_ * per:(b_ + 1) * per]:
                        g_()
                    parts[b_][1]()
                pend()
            dump("mixA", mixT[:, 4, :], RmixA)
            ckpt("p2b")
            att_locals = Rwq + Rtbp + [RQ, RK, RV, Racc, RaccH, RVT] + RsS + RsSk[0] + RsSk[1] + RPT + Rft + Rotmp

            for ssq, rstd in ((ssq_r, rstd_r), (ssq_a, rstd_a)):
                act(rstd, ssq, AF.Ln, [Rssq_r, Rssq_a], [Rrstd], scale=1.0 / 512, bias=EPS)
                act(rstd, rstd, AF.Exp, [Rrstd], [Rrstd], scale=-0.5)
            p2_locals = []
            wob = carve(H0, 4096, BF16, (8, 1024))
            Rwob = S.res("wob")
            S.alias([Rwob], Rh)
            for kt in range(8):
                dma("pool", wob[:, kt, :], wout_v[:, kt, :], [], [Rwob])
            x1 = carve(L0, 16384, F32, (16, 1024))
            Rx1 = [S.res("x1_%d" % i) for i in range(16)]
            S.alias(Rx1, att_locals)
            for tti in range(16):
                dma("sp", x1[:, tti, :], xf_d[1024 + 128 * tti:1024 + 128 * (tti + 1), :], [], [Rx1[tti]])
            for tti in range(16):
                tok = slice(128 * tti, 128 * (tti + 1))
                for half in range(2):
                    cs = slice(512 * half, 512 * (half + 1))
                    br, ba = (half * 2) % 4, (half * 2 + 1) % 4
                    br += 4 * (tti % 2); ba += 4 * (tti % 2)
                    for kt in range(4):
                        mm(banks[br][:, 0:512], mixT[:, kt, tok], wob[:, kt, cs], kt == 0, kt == 3, RmixR + [Rwob], [RB[br]], kt == 3)
                    for kt in range(4, 8):
                        mm(banks[ba][:, 0:512], mixT[:, kt, tok], wob[:, kt, cs], kt == 4, kt == 7, RmixA + [Rwob], [RB[ba]], kt == 7)
                    stt(x1[:, tti, cs], banks[br][:, 0:512], rstd_r[:, tti:tti + 1], x1[:, tti, cs], ALU.mult, ALU.add,
                        [RB[br], Rrstd, Rx1[tti]], [Rx1[tti]])
                    stt(x1[:, tti, cs], banks[ba][:, 0:512], rstd_a[:, tti:tti + 1], x1[:, tti, cs], ALU.mult, ALU.add,
                        [RB[ba], Rrstd, Rx1[tti]], [Rx1[tti]])
            dump("x1_0", x1[:, 0, :], Rx1)
            ckpt("p3")

            h2T = carve(H0 + 4096, 8192, BF16, (8, 2048))
            Rh2 = [S.res("h2T%d" % i) for i in range(4)]
            lo = L0 + 16384
            xnb2 = [carve(lo + i * 512, 512, BF16) for i in range(2)]; lo += 1024
            sqj2 = carve(lo, 512, BF16); lo += 512
            relu_t = [carve(lo + i * 512, 512) for i in range(2)]; lo += 1024
            gfin = carve(lo, 1024); lo += 1024
            uT = [carve(lo, 4096, BF16, (4, 2048)), carve(H0 + 12288, 4096, BF16, (4, 2048))]; lo += 4096
            assert lo <= AW, lo
            wup = [carve(M0_ + i * 2048, 2048, BF16, (8, 512)) for i in range(2)]
            wdn = [carve(M0_ + 4096 + i * 2048, 2048, BF16, (4, 1024)) for i in range(2)]
            Rxnb2 = [S.res("xnb2_0"), S.res("xnb2_1")]
            Rsqj2, Rgfin = S.res("sqj2"), S.res("gfin")
            Rrelu = [S.res("relu0"), S.res("relu1")]
            RuT = [S.res("uT0"), S.res("uT1")]
            Rwup = [S.res("wup0"), S.res("wup1")]
            Rwdn = [S.res("wdn0"), S.res("wdn1")]
            Rss2 = S.res("ss2")
            mix_all = RmixR + RmixA
            S.alias(Rxnb2 + [Rsqj2, Rgfin] + Rrelu + [RuT[0]], att_locals)
            S.alias(Rh2 + [RuT[1]], Rh)
            dma("sp", gfin, gfin_d, [], [Rgfin])

            wup0 = carve(H0 + 12288, 2048, BF16, (8, 512))
            Rwup0 = S.res("wup0")
            S.alias([Rwup0], Rh)

            def load_up(c):
                b = c % 2
                dst_, Rd_ = (wup0, Rwup0) if c == 0 else (wup[b], Rwup[b])
                for kt in range(8):
                    dma("pool", dst_[:, kt, :], wup_v[:, kt, 512 * c:512 * (c + 1)], [], [Rd_])

            def load_dn(c):
                b = c % 2
                for s_ in range(4):
                    dma("pool", wdn[b][:, s_, :], wdn_v[:, 4 * c + s_, :], [], [Rwdn[b]])

            S.alias(Rwup + Rwdn, mix_all)
            load_up(0)
            load_dn(0)
            load_dn(1)
            tiles = []
            for tti in range(16):
                tiles.append(dict(load=None, src=x1[:, tti, :], Rsrc=Rx1[tti], idx=tti, st=(ss, lnv, rs), xnb=xnb2[tti % 2],
                                  Rxnb=Rxnb2[tti % 2], pbank=banks[tti % 2], Rpb=RB[tti % 2], gcol0=60, dst=h2T,
                                  dstcols=slice(128 * tti, 128 * (tti + 1)), Rdst=[Rh2[tti // 4]]))
            norm_stream(tiles)
            dump("h2T0", h2T[:, 0, :], Rh2)

            def up(c):
                b = c % 2
                wsrc, Rws = (wup0, Rwup0) if c == 0 else (wup[b], Rwup[b])
                n = 0
                for s_ in range(4):
                    for tbk in range(4):
                        bk = n % 4; n += 1
                        blk = slice(512 * tbk, 512 * (tbk + 1))
                        for kt in range(8):
                            mm(banks[bk][:, 0:512], wsrc[:, kt, 128 * s_:128 * (s_ + 1)], h2T[:, kt, blk], kt == 0, kt == 7,
                               [Rws, Rh2[tbk]], [RB[bk]], kt == 7)
                        rt_, Rrt_ = relu_t[n % 2], Rrelu[n % 2]
                        act(rt_, banks[bk][:, 0:512], AF.Relu, [RB[bk]], [Rrt_])
                        act(uT[b][:, s_, blk], rt_, AF.Square, [Rrt_], [RuT[b]] + ([Rwup0] if c == 1 else []))

            def down(c):
                b = c % 2
                n = 0
                for tti in range(16):
                    tok = slice(128 * tti, 128 * (tti + 1))
                    for half in range(2):
                        bk = 4 + n % 4; n += 1
                        cs = slice(512 * half, 512 * (half + 1))
                        for s_ in range(4):
                            mm(banks[bk][:, 0:512], uT[b][:, s_, tok], wdn[b][:, s_, cs], s_ == 0, s_ == 3,
                               [RuT[b], Rwdn[b]], [RB[bk]], s_ == 3)
                        tt("dve", x1[:, tti, cs], banks[bk][:, 0:512], x1[:, tti, cs], ALU.add, [RB[bk], Rx1[tti]], [Rx1[tti]])

            for c in range(8):
                if c + 1 < 8:
                    load_up(c + 1)
                up(c)
                if c == 0:
                    dump("uT0", uT[0][:, 0, :], [RuT[0]])
                if c > 0:
                    down(c - 1)
                    if c + 1 < 8:
                        load_dn(c + 1)
            down(7)
            dump("x2_0", x1[:, 0, :], Rx1)

            for tti in range(16):
                Rst_ = S.res("fst")
                Ro_ = S.res("out%d" % tti)
                act(sqj2, x1[:, tti, :], AF.Square, [Rx1[tti]], [Rsqj2, Rst_], accum_out=ss[:, 16 + tti:17 + tti])
                act(lnv[:, 16 + tti:17 + tti], ss[:, 16 + tti:17 + tti], AF.Ln, [Rst_], [Rst_], scale=1.0 / 1024, bias=EPS)
                act(rs[:, 16 + tti:17 + tti], lnv[:, 16 + tti:17 + tti], AF.Exp, [Rst_], [Rst_], scale=-0.5)
                stt(x1[:, tti, :], x1[:, tti, :], rs[:, 16 + tti:17 + tti], gfin, ALU.mult, ALU.mult,
                    [Rx1[tti], Rst_, Rgfin], [Rx1[tti]])
                dma("sp", out_d[128 * tti:128 * (tti + 1), :], x1[:, tti, :], [Rx1[tti]], [Ro_])

        except _Stop:
            pass
        S.finish()
        S.emit()
    return nc


def _t5_bucket_np(rel):
    nb = 16
    max_exact = 8
    ret = np.where(rel > 0, nb, 0)
    n = np.abs(rel)
    nf = np.maximum(n, 1).astype(np.float32)
    large = max_exact + (np.log(nf / np.float32(max_exact)) / np.float32(math.log(1024 / max_exact))
                         * np.float32(nb - max_exact)).astype(np.int32)
    large = np.minimum(large, nb - 1)
    return ret + np.where(n < max_exact, n, large)


def _bias_index():
    i = np.arange(128)[:, None]
    jq = np.arange(128)[None, :]
    idx = np.zeros((128, 3, 256), np.int64)
    for d, p in PATS:
        for kk, sh in enumerate((-64, 64)):
            rel = i - jq + sh
            valid = np.abs(rel) <= 64
            b = _t5_bucket_np(rel * d)
            idx[:, p, kk * 128:(kk + 1) * 128] = np.where(valid, b, 32)
    return idx


def make_in_maps(inputs):
    f32 = np.float32
    x = np.asarray(inputs["x"], f32)
    w_in = np.ascontiguousarray(np.asarray(inputs["w_in"], f32)[0])
    w_out = np.ascontiguousarray(np.asarray(inputs["w_out"], f32)[0])
    w_up = np.ascontiguousarray(np.asarray(inputs["w_up"], f32)[0])
    w_down = np.ascontiguousarray(np.asarray(inputs["w_down"], f32)[0])

    def colv(v, n):
        return np.asarray(v, f32).reshape(n, 128).T

    cw = np.asarray(inputs["conv_w"], f32)[0]
    cwl = cw.reshape(4, 4, 128).transpose(2, 1, 0).reshape(128, 16)
    pvec = np.concatenate([
        colv(inputs["attn_norm_g"][0], 8), cwl, colv(inputs["conv_b"][0], 4),
        colv(inputs["lru_ba_fwd"][0], 4), colv(inputs["lru_bx_fwd"][0], 4), colv(inputs["lru_lam_fwd"][0], 4),
        colv(inputs["lru_ba_bwd"][0], 4), colv(inputs["lru_bx_bwd"][0], 4), colv(inputs["lru_lam_bwd"][0], 4),
        colv(inputs["norm_rnn_g"][0], 4), colv(inputs["norm_attn_g"][0], 4), colv(inputs["mlp_norm_g"][0], 8),
    ], axis=1).astype(f32)
    assert pvec.shape == (128, 68)
    def blockdiag(w):
        o = np.zeros((4, 128, 128), f32)
        for ct in range(4):
            o[ct, 0:64, 0:64] = w[2 * ct]
            o[ct, 64:128, 64:128] = w[2 * ct + 1]
        return o

    gsets = {}
    for dname in ("fwd", "bwd"):
        gsets[dname] = (blockdiag(np.asarray(inputs["lru_wa_" + dname], f32)[0]),
                        blockdiag(np.asarray(inputs["lru_wx_" + dname], f32)[0]))
    gw = np.zeros((128, 16, 128), f32)
    for m, (dname, which) in enumerate((("fwd", 0), ("fwd", 1), ("bwd", 0), ("bwd", 1))):
        for ct in range(4):
            gw[:, m * 4 + ct, :] = gsets[dname][which][ct]
    gw = gw.reshape(128, 16 * 128)

    def slot_params(dname):
        taps = cw if dname == "fwd" else cw[::-1]
        cwl_s = taps.reshape(4, 4, 128).transpose(2, 1, 0).reshape(128, 16)
        p = np.concatenate([cwl_s, colv(inputs["conv_b"][0], 4), colv(inputs["lru_ba_" + dname][0], 4),
                            colv(inputs["lru_bx_" + dname][0], 4), colv(inputs["lru_lam_" + dname][0], 4)], axis=1)
        g = np.zeros((128, 8, 128), f32)
        for ct in range(4):
            g[:, ct, :] = gsets[dname][0][ct]
            g[:, 4 + ct, :] = gsets[dname][1][ct]
        return p.astype(f32), g
    sp_ = {d_: slot_params(d_) for d_ in ("fwd", "bwd")}
    rb = np.asarray(inputs["rel_bias"], f32)
    tbl = np.concatenate([rb, np.full((1, 8), NEG, f32)], axis=0)
    idx = _bias_index()
    tb = tbl[idx]
    tb = tb.reshape(128, 3, 2, 128, 4, 2)
    tb = np.ascontiguousarray(tb.transpose(0, 1, 4, 2, 5, 3)).reshape(128, 3 * 8 * 256)
    ident = np.eye(128, dtype=f32)
    swp = np.zeros((128, 2, 128), f32)
    for m in range(64):
        swp[m + 64, 0, m] = 1.0
        swp[m, 1, m + 64] = 1.0
    swp = swp.reshape(128, 256)
    gfin = np.ascontiguousarray(np.broadcast_to(np.asarray(inputs["final_norm_g"], f32)[None, :], (128, 1024)))
    in_maps = []
    for c in range(NCORES):
        b, j = c // 4, c % 4
        start = 2048 * j
        xf = np.zeros((4096, 1024), f32)
        lo, hi = start - 1024, start + 3072
        slo, shi = max(lo, 0), min(hi, 8192)
        xf[slo - lo:shi - lo] = x[b, slo:shi]
        val = np.zeros((128, 3, 192), f32)
        val[:, :, 0:64] = 1.0
        val[:, :, 128:192] = 1.0
        if j == 0:
            val[0:64, 1, :] = 0.0
        if j == 3:
            val[64:128, 2, :] = 0.0
        slots = [("fwd", k) for k in range(j)] + [("bwd", k) for k in range(3, j, -1)]
        xs = np.zeros((3, 2176, 1024), f32)
        v = np.arange(-64, 2112)
        for s_, (dname, k) in enumerate(slots):
            tok = (2048 * k + v) if dname == "fwd" else (2048 * k + 2046 - v)
            ok = (tok >= 0) & (tok < 8192)
            xs[s_, ok] = x[b, tok[ok]]
        psl = np.concatenate([sp_[dname][0] for dname, _ in slots], axis=1)
        gws = np.concatenate([sp_[dname][1] for dname, _ in slots], axis=1).reshape(128, 24 * 128)
        smask = np.zeros((128, 16), f32)
        for s_ in range(1, 3):
            if slots[s_][0] == slots[s_ - 1][0]:
                smask[:, s_] = 1.0
        if j >= 1:
            smask[:, 3 + j - 1] = 1.0
        if j < 3:
            smask[:, 6 + 2] = 1.0
        in_maps.append({
            "xf": xf, "w_in": w_in, "w_out": w_out, "w_up": w_up, "w_down": w_down, "pvec": pvec, "gw": gw,
            "tb": tb, "val": val.reshape(128, 576), "ident": ident, "gfin": gfin, "swp": swp,
            "xs": xs.reshape(3 * 2176, 1024), "psl": np.ascontiguousarray(psl), "gws": np.ascontiguousarray(gws), "smask": smask,
        })
    return in_maps


_NC_CACHE = {}


def kernel(**inputs):
    in_maps = make_in_maps(inputs)
    if "nc" not in _NC_CACHE:
        _NC_CACHE["nc"] = build()
    res = run_bass_kernel_spmd(_NC_CACHE["nc"], in_maps, core_ids=list(range(NCORES)))
    out = np.zeros((2, 8192, 1024), np.float32)
    for c in range(NCORES):
        b, j = c // 4, c % 4
        out[b, 2048 * j:2048 * (j + 1)] = res.results[c]["out"]
    return out
```

```python
import math
from contextlib import ExitStack

import numpy as np
import concourse.bass as bass
import concourse.mybir as mybir
from concourse.bass_utils import run_bass_kernel_spmd

F32 = mybir.dt.float32
BF16 = mybir.dt.bfloat16
ALU = mybir.AluOpType
AF = mybir.ActivationFunctionType
AX = mybir.AxisListType

ENGS = ("pe", "act", "dve", "pool", "sp")
NEG = -30000.0
EPS = 1e-6
NCORES = 8


class Res:
    __slots__ = ("name", "w", "pw", "r", "dsem", "dcnt", "excl")

    def __init__(self, name):
        self.name = name
        self.excl = False
        self.w = None
        self.pw = []
        self.r = []
        self.dsem = None
        self.dcnt = 0


class Sched:
    def __init__(self, nc, stack):
        self.nc = nc
        self.stack = stack
        self.ops = {e: [] for e in ENGS}
        self.cnt = {e: 0 for e in ENGS}
        self.sem = {e: stack.enter_context(nc.semaphore("sem_" + e)) for e in ENGS}
        self.waited = {e: {} for e in ENGS}
        self.nsem = 0
        self.all_res = []

    def res(self, name):
        r = Res(name)
        self.all_res.append(r)
        return r

    def _need(self, eng, waits, tok, same_ok):
        if tok is None:
            return
        sem, val, peng = tok
        if peng == eng and same_ok:
            return
        if peng in self.cnt and val > self.cnt[peng]:
            raise RuntimeError("dependency on unsignaled op of %s from %s" % (peng, eng))
        key = id(sem)
        if self.waited[eng].get(key, 0) >= val:
            return
        self.waited[eng][key] = val
        waits[key] = (sem, val)

    def _deps(self, eng, reads, writes, pwrites=()):
        waits = {}
        for r in reads:
            self._need(eng, waits, r.w, same_ok=(eng == "pe"))
            for t in r.pw:
                self._need(eng, waits, t, same_ok=(eng == "pe"))
            if r.excl:
                for t in r.r:
                    self._need(eng, waits, t, same_ok=True)
        for w in writes:
            self._need(eng, waits, w.w, same_ok=True)
            for t in w.pw:
                self._need(eng, waits, t, same_ok=True)
            for t in w.r:
                self._need(eng, waits, t, same_ok=True)
        for w in pwrites:
            self._need(eng, waits, w.w, same_ok=True)
            for t in w.r:
                self._need(eng, waits, t, same_ok=True)
        return list(waits.values())

    def _commit(self, tok, reads, writes, pwrites=()):
        for r in reads:
            r.r.append(tok)
        for w in writes:
            w.w = tok
            w.pw = []
            w.r = []
        for w in pwrites:
            w.pw.append(tok)

    def op(self, eng, fn, reads=(), writes=(), signal=True, pwrites=()):
        waits = self._deps(eng, reads, writes, pwrites)
        if signal:
            self.cnt[eng] += 1
            tok = (self.sem[eng], self.cnt[eng], eng)
        else:
            tok = (self.sem[eng], self.cnt[eng] + 1, eng)
        self.ops[eng].append((waits, fn, (self.sem[eng], 1) if signal else None))
        self._commit(tok, reads, writes, pwrites)

    def dma(self, eng, fn, reads=(), writes=(), inc=16):
        waits = self._deps(eng, reads, writes)
        owner = writes[0] if writes else reads[0]
        if owner.dsem is None:
            owner.dsem = self.stack.enter_context(self.nc.semaphore("dsem_%d" % self.nsem))
            self.nsem += 1
        owner.dcnt += (inc if inc is not None else 1)
        tok = (owner.dsem, owner.dcnt, "dma")
        self.ops[eng].append((waits, fn, (owner.dsem, inc)))
        self._commit(tok, reads, writes)

    def alias(self, news, olds):
        for n in news:
            for o_ in olds:
                if o_.w is not None:
                    n.r.append(o_.w)
                n.r.extend(o_.pw)
                n.r.extend(o_.r)

    def finish(self):
        waits = {}
        for r in self.all_res:
            self._need("sp", waits, r.w, same_ok=False)
            for t in r.r + r.pw:
                self._need("sp", waits, t, same_ok=False)
        self.ops["sp"].append((list(waits.values()), None, None))

    def emit(self):
        with self.nc.Block() as block:
            def replay(name):
                def body(e):
                    for waits, fn, sig in self.ops[name]:
                        for sem, val in waits:
                            e.wait_ge(sem, val)
                        if fn is None:
                            continue
                        ins = fn(e)
                        if sig is not None:
                            if sig[1] is None:
                                ins.then_inc(sig[0])
                            else:
                                ins.then_inc(sig[0], sig[1])
                return body
            block.sync(replay("sp"))
            block.tensor(replay("pe"))
            block.scalar(replay("act"))
            block.vector(replay("dve"))
            block.gpsimd(replay("pool"))


PATS = ((1, 0), (4, 1), (16, 2))


def _key_tiles():
    tiles = {}
    info = []
    for d, p in PATS:
        M0 = 1024 // d
        nq = 2048 // d
        for r in range(d):
            for i in range(nq // 128 + 1):
                fs = r + d * (M0 - 64 + 128 * i)
                var = 1 if i == 0 else (2 if i == nq // 128 else 0)
                if d == 16:
                    var = 1 if i == 0 else 2
                tiles[(p, r, i)] = len(info)
                info.append((fs, d, var))
    return tiles, info


class _Stop(Exception):
    pass


def build(dbg=(), stop=None):
    nc = bass.Bass("TRN2", target_bir_lowering=False)

    def din(name, shape):
        return nc.dram_tensor(name, shape, F32, kind="ExternalInput").ap()

    xf_d = din("xf", [4096, 1024])
    win_d = din("w_in", [1024, 2560])
    wout_d = din("w_out", [1024, 1024])
    wup_d = din("w_up", [1024, 4096])
    wdn_d = din("w_down", [4096, 1024])
    pvec_d = din("pvec", [128, 68])
    gw_d = din("gw", [128, 16 * 128])
    tb_d = din("tb", [128, 3 * 8 * 256])
    val_d = din("val", [128, 3 * 192])
    xs_d = din("xs", [3 * 2176, 1024])
    psl_d = din("psl", [128, 96])
    gws_d = din("gws", [128, 24 * 128])
    smask_d = din("smask", [128, 16])
    swp_d = din("swp", [128, 256])
    ident_d = din("ident", [128, 128])
    gfin_d = din("gfin", [128, 1024])
    out_d = nc.dram_tensor("out", [2048, 1024], F32, kind="ExternalOutput").ap()
    dbg_d = {}
    for name, shape in dbg:
        dbg_d[name] = nc.dram_tensor("dbg_" + name, list(shape), F32, kind="ExternalOutput").ap()

    win_v = win_d.rearrange("(kt p) c -> p kt c", p=128)
    wout_v = wout_d.rearrange("(kt p) c -> p kt c", p=128)
    wup_v = wup_d.rearrange("(kt p) c -> p kt c", p=128)
    wdn_v = wdn_d.rearrange("(kt p) c -> p kt c", p=128)
    tb_v = tb_d.rearrange("q (p h c) -> q p h c", p=3, h=8)

    ktiles, kinfo = _key_tiles()
    NKT = len(kinfo)

    with ExitStack() as st:
        S = Sched(nc, st)
        AW = 53000
        arena = st.enter_context(nc.sbuf_tensor("arena", [128, AW], F32))
        banks = [st.enter_context(nc.psum_tensor("bank%d" % i, [128, 512], F32)) for i in range(8)]
        RB = [S.res("bank%d" % i) for i in range(8)]
        for r_ in RB:
            r_.excl = True

        def carve(off, nwords, dt=F32, shape=None, parts=128):
            v = arena[0:parts, off:off + nwords]
            if dt != F32:
                v = v.bitcast(dt)
            if shape is not None:
                if len(shape) == 2:
                    v = v.rearrange("p (a b) -> p a b", a=shape[0])
                elif len(shape) == 3:
                    v = v.rearrange("p (a b c) -> p a b c", a=shape[0], b=shape[1])
            return v

        o = 0
        pv = carve(o, 68); o += 80
        der = carve(o, 40); o += 48
        identb = carve(o, 64, BF16); o += 64
        ones = carve(o, 1); o += 16
        valb = carve(o, 288, BF16, (3, 192)); o += 288
        gwb = carve(o, 1024, BF16, (16, 128)); o += 1024
        cyf = carve(o, 4); o += 16
        cyb = carve(o, 4); o += 16
        ss = carve(o, 32); o += 32
        lnv = carve(o, 32); o += 32
        rs = carve(o, 32); o += 32
        ssq_r = carve(o, 16); o += 16
        ssq_a = carve(o, 16); o += 16
        rstd_r = carve(o, 16); o += 16
        rstd_a = carve(o, 16); o += 16
        swp = carve(o, 256, F32, (2, 128)); o += 256
        psl = carve(o, 96); o += 96
        ders = carve(o, 48); o += 48
        smask = carve(o, 16); o += 16
        ends = carve(o, 12); o += 16
        inits = carve(o, 12); o += 16
        ssx = carve(o, 64); o += 64
        lnx = carve(o, 64); o += 64
        rsx = carve(o, 64); o += 64
        CONST_END = 2560
        assert o <= CONST_END
        H0 = CONST_END
        M0_ = H0 + 16384
        L0 = M0_ + 8192
        LW = AW - L0
        hT = carve(H0, 16384, BF16, (8, 4096))
        mixT = carve(M0_, 8192, BF16, (8, 2048))

        Rpv, Rder, Rid, Rones, Rval, Rgw = (S.res(n) for n in ("pv", "der", "ident", "ones", "val", "gw"))
        Rcy, Rpsl, Rders, Rsm, Rends, Rinits, Rgws = (S.res(n) for n in ("cy", "psl", "ders", "smask", "ends", "inits", "gws"))
        Rss, Rssq_r, Rssq_a, Rrstd = (S.res(n) for n in ("ss", "ssq_r", "ssq_a", "rstd"))
        Rswp = S.res("swp")
        Rh = [S.res("hT%d" % i) for i in range(8)]
        RmixR = [S.res("mixR%d" % i) for i in range(4)]
        RmixA = [S.res("mixA%d" % i) for i in range(4)]
        Rout = S.res("out")
        Rdbg = S.res("dbg")

        def dma(eng, out, in_, reads, writes):
            S.dma(eng, lambda e: e.dma_start(out=out, in_=in_), reads=reads, writes=writes)

        def dump(name, ap, res):
            if name in dbg_d:
                dma("pool", dbg_d[name], ap, list(res), [Rdbg])

        def act(out, in_, func, reads, writes, pw=(), **kw):
            S.op("act", lambda e: e.activation(out=out, in_=in_, func=func, **kw), reads=reads, writes=writes, pwrites=pw)

        def mm(out, lhsT, rhs, start, stop, reads, writes, signal):
            S.op("pe", lambda e: e.matmul(out, lhsT=lhsT, rhs=rhs, start=start, stop=stop),
                 reads=reads, writes=writes, signal=signal)

        def tt(eng, out, in0, in1, op, reads, writes):
            S.op(eng, lambda e: e.tensor_tensor(out=out, in0=in0, in1=in1, op=op), reads=reads, writes=writes)

        def ts(eng, out, in0, s1, s2, op0, op1, reads, writes, pw=()):
            if s2 is None:
                S.op(eng, lambda e: e.tensor_scalar(out=out, in0=in0, scalar1=s1, scalar2=None, op0=op0),
                     reads=reads, writes=writes, pwrites=pw)
            else:
                S.op(eng, lambda e: e.tensor_scalar(out=out, in0=in0, scalar1=s1, scalar2=s2, op0=op0, op1=op1),
                     reads=reads, writes=writes, pwrites=pw)

        def stt(out, in0, scalar, in1, op0, op1, reads, writes):
            S.op("dve", lambda e: e.scalar_tensor_tensor(out=out, in0=in0, scalar=scalar, in1=in1, op0=op0, op1=op1),
                 reads=reads, writes=writes)

        def ckpt(name):
            if stop == name:
                raise _Stop()

        def hres(f0, f1):
            return [Rh[b] for b in range(f0 // 512, (f1 - 1) // 512 + 1)]

        try:
            dma("sp", pv, pvec_d, [], [Rpv])
            dma("sp", psl, psl_d, [], [Rpsl])
            dma("sp", smask, smask_d, [], [Rsm])
            dma("sp", swp, swp_d.rearrange("p (a b) -> p a b", a=2), [], [Rswp])
            dma("pool", identb, ident_d, [], [Rid])
            dma("pool", gwb, gw_d.rearrange("p (m c) -> p m c", m=16), [], [Rgw])
            dma("pool", valb, val_d.rearrange("p (m c) -> p m c", m=3), [], [Rval])
            S.op("dve", lambda e: e.memset(ones, 1.0), writes=[Rones])
            act(der[:, 0:4], pv[:, 36:40], AF.Exp, [Rpv], [Rder], scale=-1.0)
            act(der[:, 4:8], pv[:, 48:52], AF.Exp, [Rpv], [Rder], scale=-1.0)
            act(der[:, 0:8], der[:, 0:8], AF.Ln, [Rder], [Rder], bias=1.0)
            ts("dve", der[:, 0:8], der[:, 0:8], -8.0, None, ALU.mult, None, [Rder], [Rder])
            ts("dve", der[:, 8:16], der[:, 0:8], 0.5, None, ALU.mult, None, [Rder], [Rder])
            ts("dve", der[:, 16:24], der[:, 0:8], 1024.0, None, ALU.mult, None, [Rder], [Rder])
            ts("dve", der[:, 24:32], pv[:, 28:36], 0.5, None, ALU.mult, None, [Rpv, Rder], [Rder])
            ts("dve", der[:, 32:40], pv[:, 40:48], 0.5, None, ALU.mult, None, [Rpv, Rder], [Rder])

            def norm_stream(tiles):
                def stage_a(t):
                    if t["load"] is not None:
                        t["load"]()
                    ss_t, lnv_t, rs_t = t["st"]
                    i = t["idx"]
                    Rst = t["Rst"] = S.res("st")
                    act(t["xnb"], t["src"], AF.Square, [t["Rsrc"]], [t["Rxnb"], Rst], accum_out=ss_t[:, i:i + 1])
                    act(lnv_t[:, i:i + 1], ss_t[:, i:i + 1], AF.Ln, [Rst], [Rst], scale=1.0 / 1024, bias=EPS)
                    act(rs_t[:, i:i + 1], lnv_t[:, i:i + 1], AF.Exp, [Rst], [Rst], scale=-0.5)
                    ts("dve", t["xnb"], t["src"], rs_t[:, i:i + 1], None, ALU.mult, None, [t["Rsrc"], Rst], [t["Rxnb"]])

                def stage_b(t):
                    pb = t["pbank"][:].bitcast(BF16).rearrange("p (a b) -> p a b", a=8)
                    xnb_ = t["xnb"]
                    for kt in range(8):
                        S.op("pe", lambda e, kt=kt, pb=pb, xnb_=xnb_: e.transpose(out=pb[:, kt, :],
                                                                               in_=xnb_[:, kt * 128:(kt + 1) * 128],
                                                                               identity=identb),
                             reads=[t["Rxnb"], Rid], writes=[t["Rpb"]], signal=(kt == 7))
                    g0 = t["gcol0"]
                    gb = pv[:, g0:g0 + 8].unsqueeze(2).to_broadcast([128, 8, 128])
                    dst_ = t["dst"][:, :, t["dstcols"]]
                    S.op("dve", lambda e, dst_=dst_, pb=pb, gb=gb: e.tensor_tensor(out=dst_, in0=pb, in1=gb, op=ALU.mult),
                         reads=[t["Rpb"], Rpv], pwrites=t["Rdst"])

                for n in range(len(tiles) + 1):
                    if n < len(tiles):
                        stage_a(tiles[n])
                    if n >= 1:
                        stage_b(tiles[n - 1])

            xt = [carve(L0 + i * 1024, 1024) for i in range(3)]
            Rxt = [S.res("xt%d" % i) for i in range(3)]
            sqj = carve(L0 + 3072, 512, BF16)
            Rsqj = S.res("sqj")
            xnb = [carve(L0 + 3584 + i * 512, 512, BF16) for i in range(2)]
            Rxnb = [S.res("xnb%d" % i) for i in range(2)]
            order = list(range(8, 24)) + list(range(0, 8)) + list(range(24, 32))
            tiles = []
            for n, tti in enumerate(order):
                xb, Rx = xt[n % 3], Rxt[n % 3]
                tiles.append(dict(load=(lambda xb=xb, Rx=Rx, tti=tti: dma("sp", xb, xf_d[tti * 128:(tti + 1) * 128, :], [], [Rx])),
                                  src=xb, Rsrc=Rx, idx=tti, st=(ss, lnv, rs), xnb=xnb[n % 2], Rxnb=Rxnb[n % 2],
                                  pbank=banks[n % 2], Rpb=RB[n % 2], gcol0=0, dst=hT,
                                  dstcols=slice(tti * 128, (tti + 1) * 128), Rdst=[Rh[tti // 4]]))
            norm_stream(tiles)
            dump("hT0", hT[:, 0, :], Rh)
            ckpt("p1")

            for s_ in range(3):
                b0 = s_ * 32
                d0 = s_ * 16
                act(ders[:, d0 + 12:d0 + 16], psl[:, b0 + 28:b0 + 32], AF.Exp, [Rpsl], [Rders], scale=-1.0)
                act(ders[:, d0 + 12:d0 + 16], ders[:, d0 + 12:d0 + 16], AF.Ln, [Rders], [Rders], bias=1.0)
                ts("dve", ders[:, d0:d0 + 4], ders[:, d0 + 12:d0 + 16], -4.0, None, ALU.mult, None, [Rders], [Rders])
                ts("dve", ders[:, d0 + 4:d0 + 12], psl[:, b0 + 20:b0 + 28], 0.5, None, ALU.mult, None, [Rpsl, Rders], [Rders])
                ts("dve", ders[:, d0 + 12:d0 + 16], ders[:, d0 + 12:d0 + 16], -8.0, None, ALU.mult, None, [Rders], [Rders])
            gws = carve(M0_, 1536, BF16, (24, 128))
            dma("pool", gws, gws_d.rearrange("p (m c) -> p m c", m=24), [], [Rgws])
            wrg = carve(M0_ + 4096, 4096, BF16, (8, 1024))
            Rwrg = S.res("wrg")
            for kt in range(8):
                dma("pool", wrg[:, kt, :], win_v[:, kt, 0:1024], [], [Rwrg])
            lo = L0 + 4608
            xr = carve(lo, 2064); lo += 2064
            xc2 = [carve(lo, 2048), carve(lo + 2048, 2048)]; lo += 4096
            rta = carve(lo, 2048); lo += 2048
            itb = carve(lo, 2048); lo += 2048
            sqs = carve(lo, 2048); lo += 2048
            hTs = carve(lo, 8704, BF16, (8, 2176))
            gg = carve(lo, 2048)
            ysq = carve(lo + 2048, 2048)
            hl = [carve(lo + 4096, 2048), carve(lo + 6144, 2048)]
            lo += 8704
            assert lo <= AW, lo
            xcb2 = [carve(M0_ + 2048, 1024, BF16), carve(M0_ + 3072, 1024, BF16)]
            Rxrp = [S.res("xr%d" % i) for i in range(6)]
            Rxc2 = [[S.res("xc%d_%d" % (i, t)) for t in range(4)] for i in range(2)]
            Rxcb2 = [[S.res("xcb%d_%d" % (i, t)) for t in range(4)] for i in range(2)]
            Rrta = [S.res("rta%d" % t) for t in range(4)]
            Ritb = [S.res("itb%d" % t) for t in range(4)]
            Rsqs = [S.res("sqs%d" % t) for t in range(4)]
            RhTs, Rgg, Rysq = S.res("hTs"), S.res("gg"), S.res("ysq")
            Rhl = [S.res("hl0"), S.res("hl1")]
            B4 = [slice(t * 512, (t + 1) * 512) for t in range(4)]

            def conv_only(ct, buf, cwcol, cbcol, src_pv, Rsrc_pv):
                xc = xc2[buf]
                rx = [[Rxrp[t], Rxrp[t + 1], Rxrp[t + 2]] for t in range(4)]
                for t in range(4):
                    c0 = t * 512
                    ts("dve", xc[:, B4[t]], xr[:, c0:c0 + 512], src_pv[:, cwcol + ct * 4:cwcol + ct * 4 + 1],
                       src_pv[:, cbcol + ct:cbcol + ct + 1], ALU.mult, ALU.add, rx[t] + [Rsrc_pv], [Rxc2[buf][t]])
                for k in range(1, 4):
                    for t in range(4):
                        c0 = t * 512
                        stt(xc[:, B4[t]], xr[:, c0 + k:c0 + k + 512], src_pv[:, cwcol + ct * 4 + k:cwcol + ct * 4 + k + 1],
                            xc[:, B4[t]], ALU.mult, ALU.add, rx[t] + [Rsrc_pv, Rxc2[buf][t]], [Rxc2[buf][t]])

            def cast_only(buf):
                xc, xcb = xc2[buf], xcb2[buf]
                for t in range(4):
                    S.op("act", lambda e, blk=B4[t], xc=xc, xcb=xcb: e.copy(out=xcb[:, blk], in_=xc[:, blk]),
                         reads=[Rxc2[buf][t]], writes=[Rxcb2[buf][t]])

            def gate_pieces(buf, gA, gX, Rg, hbA, hbX, hc, c2, Rp, init_fn, reverse, hout, Rhout, sq_on_pool=False):
                xc, xcb = xc2[buf], xcb2[buf]

                def g1():
                    for t in range(4):
                        blk = B4[t]
                        ba_, bx_ = 4 + (t % 2) * 2, 5 + (t % 2) * 2
                        mm(banks[ba_][:, 0:512], gA, xcb[:, blk], True, True, [Rg, Rxcb2[buf][t]], [RB[ba_]], True)
                        mm(banks[bx_][:, 0:512], gX, xcb[:, blk], True, True, [Rg, Rxcb2[buf][t]], [RB[bx_]], True)
                        act(rta[:, blk], banks[ba_][:, 0:512], AF.Tanh, [RB[ba_], Rp], [Rrta[t]], scale=0.5, bias=hbA)
                        act(itb[:, blk], banks[bx_][:, 0:512], AF.Tanh, [RB[bx_], Rp], [Ritb[t]], scale=0.5, bias=hbX)

                def g2():
                    for t in range(4):
                        if sq_on_pool:
                            act(rta[:, B4[t]], rta[:, B4[t]], AF.Exp, [Rrta[t], Rp], [Rrta[t]], scale=hc, bias=hc)
                            tt("pool", sqs[:, B4[t]], rta[:, B4[t]], rta[:, B4[t]], ALU.mult, [Rrta[t]], [Rsqs[t]])
                        else:
                            act(sqs[:, B4[t]], rta[:, B4[t]], AF.Exp, [Rrta[t], Rp], [Rsqs[t]], scale=c2, bias=c2)
                            act(rta[:, B4[t]], rta[:, B4[t]], AF.Exp, [Rrta[t], Rp], [Rrta[t]], scale=hc, bias=hc)

                def g3():
                    for t in range(4):
                        blk = B4[t]
                        stt(itb[:, blk], itb[:, blk], 1.0, xc[:, blk], ALU.add, ALU.mult, [Ritb[t], Rxc2[buf][t]], [Ritb[t]])

                def g4():
                    for t in range(4):
                        blk = B4[t]
                        act(sqs[:, blk], sqs[:, blk], AF.Sqrt, [Rsqs[t]], [Rsqs[t]], scale=-0.25, bias=0.25)
                        tt("pool", itb[:, blk], itb[:, blk], sqs[:, blk], ALU.mult, [Ritb[t], Rsqs[t]], [Ritb[t]])

                def g5():
                    init, Rinit = init_fn()
                    order_ = range(3, -1, -1) if reverse else range(4)
                    prev = None
                    for t in order_:
                        blk = B4[t]
                        if prev is None:
                            ini, Ri = init, Rinit
                        elif reverse:
                            ini, Ri = hout[:, 512 * (t + 1):512 * (t + 1) + 1], Rhout
                        else:
                            ini, Ri = hout[:, 512 * t - 1:512 * t], Rhout
                        if reverse:
                            S.op("dve", lambda e, blk=blk, ini=ini: e.tensor_tensor_scan(
                                out=hout[:, blk][:, ::-1], data0=rta[:, blk][:, ::-1], data1=itb[:, blk][:, ::-1],
                                initial=ini, op0=ALU.mult, op1=ALU.add),
                                 reads=[Rrta[t], Ritb[t]] + Ri, writes=[Rhout[t]] if len(Rhout) == 4 else Rhout)
                        else:
                            S.op("dve", lambda e, blk=blk, ini=ini: e.tensor_tensor_scan(
                                out=hout[:, blk], data0=rta[:, blk], data1=itb[:, blk],
                                initial=ini, op0=ALU.mult, op1=ALU.add),
                                 reads=[Rrta[t], Ritb[t]] + Ri, writes=[Rhout[t]] if len(Rhout) == 4 else Rhout)
                        prev = t
                return g1, g2, g3, g4, g5

            jobn = [0]

            def slot_job(s_, ct):
                buf = jobn[0] % 2
                jobn[0] += 1
                wx = slice(ct * 128, (ct + 1) * 128)
                d0 = s_ * 16

                def f1():
                    for bi, (fc0, n_) in enumerate(((62, 512), (574, 512), (1086, 512), (1598, 512), (2110, 4))):
                        bk = bi % 2
                        for kt in range(8):
                            mm(banks[bk][:, 0:n_], wrg[:, kt, wx], hTs[:, kt, fc0:fc0 + n_], kt == 0, kt == 7,
                               [Rwrg, RhTs], [RB[bk]], kt == 7)
                        act(xr[:, fc0 - 62:fc0 - 62 + n_], banks[bk][:, 0:n_], AF.Copy, [RB[bk]], [Rxrp[bi + 1]])

                def f2():
                    conv_only(ct, buf, s_ * 32, s_ * 32 + 16, psl, Rpsl)

                def f3():
                    cast_only(buf)

                def init_fn():
                    if s_ == 0:
                        return 0.0, []
                    ts("dve", inits[:, s_ * 4 + ct:s_ * 4 + ct + 1], ends[:, (s_ - 1) * 4 + ct:(s_ - 1) * 4 + ct + 1],
                       smask[:, s_:s_ + 1], None, ALU.mult, None, [Rends, Rsm], [Rinits])
                    return inits[:, s_ * 4 + ct:s_ * 4 + ct + 1], [Rinits]

                g = gate_pieces(buf, gws[:, s_ * 8 + ct, :], gws[:, s_ * 8 + 4 + ct, :], Rgws,
                                ders[:, d0 + 4 + ct:d0 + 5 + ct], ders[:, d0 + 8 + ct:d0 + 9 + ct], ders[:, d0 + ct:d0 + ct + 1],
                                ders[:, d0 + 12 + ct:d0 + 13 + ct], Rders, init_fn, False, sqs, Rsqs)

                def fin():
                    S.op("dve", lambda e: e.tensor_copy(out=ends[:, s_ * 4 + ct:s_ * 4 + ct + 1], in_=sqs[:, 2047:2048]),
                         reads=[Rsqs[3]], writes=[Rends])
                return dict(f=(f1, f2, f3), g=[g], fin=fin, pre=None)

            def own_job(ct):
                buf = jobn[0] % 2
                jobn[0] += 1
                wx = slice(ct * 128, (ct + 1) * 128)
                wg = slice(512 + ct * 128, 512 + (ct + 1) * 128)

                def f1():
                    for t in range(4):
                        bk = t % 2
                        f0 = 1024 + 512 * t
                        for kt in range(8):
                            mm(banks[bk][:, 0:512], wrg[:, kt, wx], hT[:, kt, f0:f0 + 512], kt == 0, kt == 7,
                               [Rwrg] + hres(f0, f0 + 512), [RB[bk]], kt == 7)
                        act(xr[:, 2 + 512 * t:2 + 512 * (t + 1)], banks[bk][:, 0:512], AF.Copy, [RB[bk]], [Rxrp[1 + t]])
                    for kt in range(8):
                        mm(banks[2][:, 0:2], wrg[:, kt, wx], hT[:, kt, 1022:1024], kt == 0, kt == 7, [Rwrg, Rh[1]], [RB[2]], False)
                    for kt in range(8):
                        mm(banks[2][:, 2:4], wrg[:, kt, wx], hT[:, kt, 3072:3074], kt == 0, kt == 7, [Rwrg, Rh[6]], [RB[2]], kt == 7)
                    act(xr[:, 0:2], banks[2][:, 0:2], AF.Copy, [RB[2]], [Rxrp[0]])
                    act(xr[:, 2050:2052], banks[2][:, 2:4], AF.Copy, [RB[2]], [Rxrp[5]])

                def f2():
                    conv_only(ct, buf, 8, 24, pv, Rpv)

                def f3():
                    cast_only(buf)

                def pre():
                    if ct == 0:
                        carries()
                    for t in range(4):
                        bk = 2 + t % 2
                        f0 = 1024 + 512 * t
                        for kt in range(8):
                            mm(banks[bk][:, 0:512], wrg[:, kt, wg], hT[:, kt, f0:f0 + 512], kt == 0, kt == 7,
                               [Rwrg] + hres(f0, f0 + 512), [RB[bk]], kt == 7)
                        act(gg[:, B4[t]], banks[bk][:, 0:512], AF.Gelu_apprx_tanh, [RB[bk]], [Rgg])
                    if ct == 0:
                        dump("xc0", xc2[buf], Rxc2[buf])
                        dump("gg0", gg, [Rgg])

                gs = []
                for d in range(2):
                    cy = cyf if d == 0 else cyb
                    gs.append(gate_pieces(buf, gwb[:, (2 * d) * 4 + ct, :], gwb[:, (2 * d + 1) * 4 + ct, :], Rgw,
                                          der[:, 24 + 8 * d + ct:25 + 8 * d + ct], der[:, 28 + 8 * d + ct:29 + 8 * d + ct],
                                          der[:, 8 + 4 * d + ct:9 + 4 * d + ct], der[:, 4 * d + ct:4 * d + ct + 1], Rder,
                                          (lambda cy=cy: (cy[:, ct:ct + 1], [Rcy])), d == 1, hl[d], [Rhl[d]], sq_on_pool=True))

                def fin():
                    y = hl[0]
                    tt("dve", y, hl[0], hl[1], ALU.add, [Rhl[0], Rhl[1]], [Rhl[0]])
                    tt("pool", y, y, gg, ALU.mult, [Rhl[0], Rgg], [Rhl[0]])
                    if ct == 0:
                        dump("y0", y, [Rhl[0]])
                    act(ysq, y, AF.Square, [Rhl[0]], [Rysq])
                    ts("dve", mixT[:, ct, :], y, pv[:, 52 + ct:53 + ct], None, ALU.mult, None, [Rhl[0], Rpv],
                       [RmixR[ct]] + (Rxcb2[buf] if ct >= 2 else []))

                def fin_stats():
                    for i in range(16):
                        mm(banks[7][:, i:i + 1], ysq[:, 128 * i:128 * (i + 1)], ones, True, True, [Rysq, Rones], [RB[7]], i == 15)

                def fin_ssq():
                    if ct == 0:
                        S.op("dve", lambda e: e.tensor_copy(out=ssq_r, in_=banks[7][:, 0:16]), reads=[RB[7]], writes=[Rssq_r])
                    else:
                        tt("dve", ssq_r, banks[7][:, 0:16], ssq_r, ALU.add, [RB[7], Rssq_r], [Rssq_r])
                return dict(f=(f1, f2, f3), g=gs, fin=fin, pre=pre, fin_stats=fin_stats, fin_ssq=fin_ssq)

            def slot_norm(s_):
                tiles = []
                for tti in range(17):
                    n = s_ * 17 + tti
                    xb, Rx = xt[n % 3], Rxt[n % 3]
                    r0 = s_ * 2176 + tti * 128
                    tiles.append(dict(load=(lambda xb=xb, Rx=Rx, r0=r0: dma("sp", xb, xs_d[r0:r0 + 128, :], [], [Rx])),
                                      src=xb, Rsrc=Rx, idx=n, st=(ssx, lnx, rsx), xnb=xnb[n % 2], Rxnb=Rxnb[n % 2],
                                      pbank=banks[2 + n % 2], Rpb=RB[2 + n % 2], gcol0=0, dst=hTs,
                                      dstcols=slice(tti * 128, (tti + 1) * 128), Rdst=[RhTs]))
                norm_stream(tiles)

            def carries():
                dump("ends", ends, [Rends])
                S.op("dve", lambda e: e.memset(cyf, 0.0), writes=[Rcy])
                S.op("dve", lambda e: e.memset(cyb, 0.0), writes=[Rcy])
                for s_ in range(3):
                    stt(cyf, ends[:, s_ * 4:s_ * 4 + 4], smask[:, 3 + s_:4 + s_], cyf, ALU.mult, ALU.add, [Rends, Rsm, Rcy], [Rcy])
                    stt(cyb, ends[:, s_ * 4:s_ * 4 + 4], smask[:, 6 + s_:7 + s_], cyb, ALU.mult, ALU.add, [Rends, Rsm, Rcy], [Rcy])

            specs = [("slot", s_, ct) for s_ in range(3) for ct in range(4)] + [("own", None, ct) for ct in range(4)]

            def make(spec):
                kind, s_, ct = spec
                if kind == "slot":
                    if ct == 0:
                        slot_norm(s_)
                    return slot_job(s_, ct)
                if ct == 0:
                    S.alias([Rgg, Rysq] + Rhl, [RhTs])
                    S.alias(RmixR, [Rgws])
                if ct == 2:
                    early_pair0()
                return own_job(ct)

            wq = [carve(L0 + i * 1536, 1536, BF16, (8, 384)) for i in range(2)]
            tbp = [carve(L0 + 3072 + i * 1536, 1536, F32, (3, 2, 256)) for i in range(2)]
            Rwq = [S.res("wq0"), S.res("wq1")]
            Rtbp = [S.res("tbp0"), S.res("tbp1")]

            def load_pair(hp):
                b = hp % 2
                for i, c0 in enumerate((1024, 1536, 2048)):
                    dma("pool", wq[b][:, :, i * 128:(i + 1) * 128], win_v[:, :, c0 + hp * 128:c0 + (hp + 1) * 128],
                        [], [Rwq[b]])
                dma("sp", tbp[b], tb_v[:, :, 2 * hp:2 * hp + 2, :], [], [Rtbp[b]])

            def early_pair0():
                S.alias(Rwq + [Rtbp[0]], Rxt + Rxnb + [Rsqj])
                load_pair(0)

            jobs = {}

            def job(i):
                if i >= len(specs):
                    return None
                if i not in jobs:
                    jobs[i] = make(specs[i])
                return jobs[i]

            pend_stats, pend_ssq = [None], [None]
            j0 = job(0)
            for f in j0["f"]:
                f()
            if job(1) is not None:
                job(1)["f"][0]()
            for i in range(len(specs)):
                cur, n1, n2 = job(i), job(i + 1), job(i + 2)
                if cur["pre"] is not None:
                    cur["pre"]()
                gl = cur["g"]
                g1, g2, g3, g4, g5 = gl[0]
                if n1 is not None:
                    n1["f"][1]()
                g1()
                if pend_stats[0] is not None:
                    pend_stats[0]()
                    pend_stats[0] = None
                g2()
                g3()
                if pend_ssq[0] is not None:
                    pend_ssq[0]()
                    pend_ssq[0] = None
                if n1 is not None:
                    n1["f"][2]()
                g4()
                if n2 is not None:
                    n2["f"][0]()
                g5()
                if len(gl) == 2:
                    h1, h2, h3, h4, h5 = gl[1]
                    h1(); h2(); h3(); h4(); h5()
                cur["fin"]()
                if "fin_stats" in cur:
                    if n1 is not None:
                        pend_stats[0], pend_ssq[0] = cur["fin_stats"], cur["fin_ssq"]
                    else:
                        cur["fin_stats"]()
                        cur["fin_ssq"]()
            ckpt("rnn")
            lo = L0
            lo += 6144
            Qp = carve(lo, 1024, BF16); lo += 1024
            Kp = carve(lo, 2048, BF16); lo += 2048
            Vp = carve(lo, NKT * 96, BF16, (NKT, 192)); lo += NKT * 96
            acc = carve(lo, 4096, F32, (2, 2048)); lo += 4096
            NU = 3
            sS = [carve(lo + i * 512, 512, F32, (2, 256)) for i in range(NU)]; lo += 512 * NU
            PT = [carve(lo + i * 256, 256, BF16, (2, 256)) for i in range(NU)]; lo += 256 * NU
            ftmp = [carve(lo + i * 512, 512) for i in range(2)]; lo += 1024
            VT = carve(lo, 2048, BF16); lo += 2048
            RVT = S.res("VT")
            otmp = [carve(lo + i * 256, 256, F32, (2, 128)) for i in range(2)]; lo += 512
            Rotmp = [S.res("otmp0"), S.res("otmp1")]
            assert lo <= AW, lo
            RQ, RK, RV, Racc = S.res("Qp"), S.res("Kp"), S.res("Vp"), S.res("accL")
            RaccH = S.res("accH")
            RsS = [S.res("sS%d" % i) for i in range(NU)]
            RPT = [S.res("PT%d" % i) for i in range(NU)]
            RsSh = [[S.res("sS%d_%d" % (i, h)) for h in range(2)] for i in range(NU)]
            RPTh = [[S.res("PT%d_%d" % (i, h)) for h in range(2)] for i in range(NU)]
            Rft = [S.res("ft%d" % i) for i in range(2)]
            old_locals = (Rxrp + Rxc2[0] + Rxc2[1] + [Rgg] + Rrta + Ritb + Rsqs + [RhTs, Rysq, Rsqj] + Rhl + Rxt + Rxnb)
            S.alias(Rwq + Rtbp + [RQ, RK, RV, Racc, RaccH, RVT] + RsS + RPT + [r_ for l_ in RsSh + RPTh for r_ in l_] + Rft + Rotmp, old_locals)

            units = []
            for d, p in PATS:
                for r in range(d):
                    for qi in range(2048 // d // 128):
                        units.append((p, d, r, qi))
            Vv = Vp.rearrange("p t (b c) -> p t b c", b=3)
            for ti_ in range(NKT):
                var_ = kinfo[ti_][2]
                S.op("pool", lambda e, ti_=ti_, var_=var_: e.tensor_copy(out=Vp[:, ti_, 64:128], in_=valb[:, var_, 0:64]),
                     reads=[Rval], pwrites=[RV])
            pbank = [0]

            def nb():
                bk = pbank[0] % 6
                pbank[0] += 1
                return bk

            def proj_groups(hp):
                wb = hp % 2
                groups = []

                def gq(tbk):
                    bk = nb()
                    f0 = 1024 + 512 * tbk
                    for kt in range(8):
                        mm(banks[bk][:, 0:512], wq[wb][:, kt, 0:128], hT[:, kt, f0:f0 + 512], kt == 0, kt == 7,
                           [Rwq[wb]] + hres(f0, f0 + 512), [RB[bk]], kt == 7)
                    S.op("act", lambda e: e.copy(out=Qp[:, 512 * tbk:512 * (tbk + 1)], in_=banks[bk][:, 0:512]),
                         reads=[RB[bk]], pwrites=[RQ])

                def gk(fb):
                    bk = nb()
                    f0 = 512 * fb
                    for kt in range(8):
                        mm(banks[bk][:, 0:512], wq[wb][:, kt, 128:256], hT[:, kt, f0:f0 + 512], kt == 0, kt == 7,
                           [Rwq[wb], Rh[fb]], [RB[bk]], kt == 7)
                    S.op("dve", lambda e: e.tensor_copy(out=Kp[:, 512 * fb:512 * (fb + 1)], in_=banks[bk][:, 0:512]),
                         reads=[RB[bk]], pwrites=[RK])

                def gv(fb):
                    bk = nb()
                    f0 = 512 * fb
                    for kt in range(8):
                        mm(banks[bk][:, 0:512], wq[wb][:, kt, 256:384], hT[:, kt, f0:f0 + 512], kt == 0, kt == 7,
                           [Rwq[wb], Rh[fb]], [RB[bk]], kt == 7)
                    S.op("act", lambda e: e.copy(out=VT[:, 512 * fb:512 * (fb + 1)], in_=banks[bk][:, 0:512]),
                         reads=[RB[bk]], pwrites=[RVT])

                def gt(t0):
                    bk = nb()
                    nt = min(8, NKT - t0)
                    pbv = banks[bk][:].bitcast(BF16).rearrange("p (a b) -> p a b", a=8)
                    for ti in range(nt):
                        fs, d, _ = kinfo[t0 + ti]
                        S.op("pe", lambda e, ti=ti, fs=fs, d=d: e.transpose(out=pbv[:, ti, :], in_=VT[:, fs:fs + 127 * d + 1:d],
                                                                          identity=identb),
                             reads=[RVT, Rid], writes=[RB[bk]], signal=(ti == nt - 1))
                    src = pbv[:, 0:nt, :].rearrange("p t (b c) -> p t b c", b=2)
                    if (t0 // 8) % 2 == 0:
                        S.op("act", lambda e: e.copy(out=Vv[:, t0:t0 + nt, 0:3:2, :], in_=src), reads=[RB[bk]], pwrites=[RV])
                    else:
                        S.op("dve", lambda e: e.tensor_copy(out=Vv[:, t0:t0 + nt, 0:3:2, :], in_=src),
                             reads=[RB[bk]], pwrites=[RV])

                for tbk in range(4):
                    groups.append(lambda tbk=tbk: gq(tbk))
                for fb in range(8):
                    groups.append(lambda fb=fb: gk(fb))
                for fb in range(8):
                    groups.append(lambda fb=fb: gv(fb))
                for t0 in range(0, NKT, 8):
                    groups.append(lambda t0=t0: gt(t0))
                return groups

            def run_units(hp):
                wb = hp % 2

                def qk(u, un):
                    p, d, r, qi = un
                    sb = (u % NU) * 2
                    t0i, t1i = ktiles[(p, r, qi)], ktiles[(p, r, qi + 1)]
                    q0 = r + 128 * d * qi
                    for kk, ti in enumerate((t0i, t1i)):
                        fs = kinfo[ti][0]
                        for hh in range(2):
                            ps = slice(64 * hh, 64 * hh + 64)
                            mm(banks[sb + hh][:, kk * 128:(kk + 1) * 128], Kp[ps, fs:fs + 127 * d + 1:d],
                               Qp[ps, q0:q0 + 127 * d + 1:d], True, True, [RK, RQ], [RB[sb + hh]], kk == 1)
                    for hh in range(2):
                        stt(sS[u % NU][:, hh, :], banks[sb + hh][:, 0:256], 0.125, tbp[wb][:, p, hh, :], ALU.mult, ALU.add,
                            [RB[sb + hh], Rtbp[wb]], [RsSh[u % NU][hh]])
                        act(PT[u % NU][:, hh, :], sS[u % NU][:, hh, :], AF.Exp, [RsSh[u % NU][hh]], [RPTh[u % NU][hh]])

                def pv_(u, un):
                    p, d, r, qi = un
                    ob = (u % NU) * 2
                    t0i, t1i = ktiles[(p, r, qi)], ktiles[(p, r, qi + 1)]
                    q0 = r + 128 * d * qi
                    for hh in range(2):
                        for kk, ti in enumerate((t0i, t1i)):
                            mm(banks[ob][:, 256 + 128 * hh:256 + 128 * (hh + 1)], Vp[:, ti, 64 * hh:64 * hh + 128],
                               PT[u % NU][:, hh, kk * 128:(kk + 1) * 128], kk == 0, kk == 1, [RV, RPTh[u % NU][hh]], [RB[ob]],
                               (hh == 1 and kk == 1))
                    src = banks[ob][:, 256:512].rearrange("p (a b) -> p a b", a=2)
                    dst = acc[:, :, q0:q0 + 127 * d + 1:d]
                    if p == 0:
                        S.op("act", lambda e: e.copy(out=dst, in_=src), reads=[RB[ob]], writes=[Racc])
                    else:
                        tmp_, Rt_ = otmp[u % 2], Rotmp[u % 2]
                        S.op("act", lambda e: e.copy(out=tmp_, in_=src), reads=[RB[ob]], writes=[Rt_])
                        tt("pool", dst, tmp_, dst, ALU.add, [Rt_, Racc], [Racc])

                for u, un in enumerate(units):
                    qk(u, un)
                    if u >= 1:
                        pv_(u - 1, units[u - 1])
                pv_(len(units) - 1, units[-1])

            def fin_parts(hp):
                parts = []
                f_rec, f_o = ftmp[0], ftmp[1]
                Rrec, Ro = Rft[0], Rft[1]

                def p1(tbk):
                    blk = slice(512 * tbk, 512 * (tbk + 1))
                    mm(banks[7][:, 0:512], swp[:, 0, :], acc[:, 0, blk], True, False, [Rswp, Racc], [RB[7]], False)
                    mm(banks[7][:, 0:512], swp[:, 1, :], acc[:, 1, blk], False, True, [Rswp, Racc], [RB[7]], True)
                    act(f_rec, banks[7][:, 0:512], AF.Ln, [RB[7]], [Rrec])
                    act(f_rec, f_rec, AF.Exp, [Rrec], [Rrec], scale=-1.0)
                    tt("dve", f_o[0:64, :], acc[0:64, 0, blk], f_rec[0:64, :], ALU.mult, [Racc, Rrec], [Ro])
                    tt("dve", f_o[64:128, :], acc[64:128, 1, blk], f_rec[64:128, :], ALU.mult, [Racc, Rrec, Ro], [Ro])
                    ts("dve", mixT[:, 4 + hp, blk], f_o, pv[:, 56 + hp:57 + hp], None, ALU.mult, None, [Ro, Rpv], [RmixA[hp]])
                    act(f_rec, f_o, AF.Square, [Ro], [Rrec])

                def p2(tbk):
                    for i in range(4):
                        col = 4 * tbk + i
                        mm(banks[6][:, col:col + 1], f_rec[:, 128 * i:128 * (i + 1)], ones, True, True, [Rrec, Rones], [RB[6]],
                           i == 3)

                def pend():
                    if hp == 0:
                        S.op("dve", lambda e: e.tensor_copy(out=ssq_a, in_=banks[6][:, 0:16]), reads=[RB[6]], writes=[Rssq_a])
                    else:
                        tt("dve", ssq_a, banks[6][:, 0:16], ssq_a, ALU.add, [RB[6], Rssq_a], [Rssq_a])
                for tbk in range(4):
                    parts.append((lambda tbk=tbk: p1(tbk), lambda tbk=tbk: p2(tbk)))
                return parts, pend

            load_pair(1)
            for g_ in proj_groups(0):
                g_()
            for hp in range(4):
                if 1 <= hp < 3:
                    load_pair(hp + 1)
                run_units(hp)
                parts, pend = fin_parts(hp)
                nxt = proj_groups(hp + 1) if hp + 1 < 4 else []
                per = (len(nxt) + 3) // 4
                for b_ in range(4):
                    parts[b_][0]()
                    for g_ in nxt[b_ * per:(b_ + 1) * per]:
                        g_()
                    parts[b_][1]()
                pend()
            dump("mixA", mixT[:, 4, :], RmixA)
            ckpt("p2b")
            att_locals = Rwq + Rtbp + [RQ, RK, RV, Racc, RaccH, RVT] + RsS + RPT + [r_ for l_ in RsSh + RPTh for r_ in l_] + Rft + Rotmp

            for ssq, rstd in ((ssq_r, rstd_r), (ssq_a, rstd_a)):
                act(rstd, ssq, AF.Ln, [Rssq_r, Rssq_a], [Rrstd], scale=1.0 / 512, bias=EPS)
                act(rstd, rstd, AF.Exp, [Rrstd], [Rrstd], scale=-0.5)
            p2_locals = []
            wob = carve(H0, 4096, BF16, (8, 1024))
            Rwob = S.res("wob")
            S.alias([Rwob], Rh)
            for kt in range(8):
                dma("pool", wob[:, kt, :], wout_v[:, kt, :], [], [Rwob])
            x1 = carve(L0, 16384, F32, (16, 1024))
            Rx1 = [S.res("x1_%d" % i) for i in range(16)]
            S.alias(Rx1, att_locals)
            for tti in range(16):
                dma("sp", x1[:, tti, :], xf_d[1024 + 128 * tti:1024 + 128 * (tti + 1), :], [], [Rx1[tti]])
            for tti in range(16):
                tok = slice(128 * tti, 128 * (tti + 1))
                for half in range(2):
                    cs = slice(512 * half, 512 * (half + 1))
                    br, ba = (half * 2) % 4, (half * 2 + 1) % 4
                    br += 4 * (tti % 2); ba += 4 * (tti % 2)
                    for kt in range(4):
                        mm(banks[br][:, 0:512], mixT[:, kt, tok], wob[:, kt, cs], kt == 0, kt == 3, RmixR + [Rwob], [RB[br]], kt == 3)
                    for kt in range(4, 8):
                        mm(banks[ba][:, 0:512], mixT[:, kt, tok], wob[:, kt, cs], kt == 4, kt == 7, RmixA + [Rwob], [RB[ba]], kt == 7)
                    stt(x1[:, tti, cs], banks[br][:, 0:512], rstd_r[:, tti:tti + 1], x1[:, tti, cs], ALU.mult, ALU.add,
                        [RB[br], Rrstd, Rx1[tti]], [Rx1[tti]])
                    stt(x1[:, tti, cs], banks[ba][:, 0:512], rstd_a[:, tti:tti + 1], x1[:, tti, cs], ALU.mult, ALU.add,
                        [RB[ba], Rrstd, Rx1[tti]], [Rx1[tti]])
            dump("x1_0", x1[:, 0, :], Rx1)
            ckpt("p3")

            h2T = carve(H0 + 4096, 8192, BF16, (8, 2048))
            Rh2 = [S.res("h2T%d" % i) for i in range(4)]
            lo = L0 + 16384
            xnb2 = [carve(lo + i * 512, 512, BF16) for i in range(2)]; lo += 1024
            sqj2 = carve(lo, 512, BF16); lo += 512
            relu_t = [carve(lo + i * 512, 512) for i in range(2)]; lo += 1024
            gfin = carve(lo, 1024); lo += 1024
            uT = [carve(lo, 4096, BF16, (4, 2048)), carve(H0 + 12288, 4096, BF16, (4, 2048))]; lo += 4096
            assert lo <= AW, lo
            wup = [carve(M0_ + i * 2048, 2048, BF16, (8, 512)) for i in range(2)]
            wdn = [carve(M0_ + 4096 + i * 2048, 2048, BF16, (4, 1024)) for i in range(2)]
            Rxnb2 = [S.res("xnb2_0"), S.res("xnb2_1")]
            Rsqj2, Rgfin = S.res("sqj2"), S.res("gfin")
            Rrelu = [S.res("relu0"), S.res("relu1")]
            RuT = [S.res("uT0"), S.res("uT1")]
            Rwup = [S.res("wup0"), S.res("wup1")]
            Rwdn = [S.res("wdn0"), S.res("wdn1")]
            Rss2 = S.res("ss2")
            mix_all = RmixR + RmixA
            S.alias(Rxnb2 + [Rsqj2, Rgfin] + Rrelu + [RuT[0]], att_locals)
            S.alias(Rh2 + [RuT[1]], Rh)
            dma("sp", gfin, gfin_d, [], [Rgfin])

            wup0 = carve(H0 + 12288, 2048, BF16, (8, 512))
            Rwup0 = S.res("wup0")
            S.alias([Rwup0], Rh)

            def load_up(c):
                b = c % 2
                dst_, Rd_ = (wup0, Rwup0) if c == 0 else (wup[b], Rwup[b])
                for kt in range(8):
                    dma("pool", dst_[:, kt, :], wup_v[:, kt, 512 * c:512 * (c + 1)], [], [Rd_])

            def load_dn(c):
                b = c % 2
                for s_ in range(4):
                    dma("pool", wdn[b][:, s_, :], wdn_v[:, 4 * c + s_, :], [], [Rwdn[b]])

            S.alias(Rwup + Rwdn, mix_all)
            load_up(0)
            load_dn(0)
            load_dn(1)
            tiles = []
            for tti in range(16):
                tiles.append(dict(load=None, src=x1[:, tti, :], Rsrc=Rx1[tti], idx=tti, st=(ss, lnv, rs), xnb=xnb2[tti % 2],
                                  Rxnb=Rxnb2[tti % 2], pbank=banks[tti % 2], Rpb=RB[tti % 2], gcol0=60, dst=h2T,
                                  dstcols=slice(128 * tti, 128 * (tti + 1)), Rdst=[Rh2[tti // 4]]))
            norm_stream(tiles)
            dump("h2T0", h2T[:, 0, :], Rh2)

            def up(c):
                b = c % 2
                wsrc, Rws = (wup0, Rwup0) if c == 0 else (wup[b], Rwup[b])
                n = 0
                for s_ in range(4):
                    for tbk in range(4):
                        bk = n % 4; n += 1
                        blk = slice(512 * tbk, 512 * (tbk + 1))
                        for kt in range(8):
                            mm(banks[bk][:, 0:512], wsrc[:, kt, 128 * s_:128 * (s_ + 1)], h2T[:, kt, blk], kt == 0, kt == 7,
                               [Rws, Rh2[tbk]], [RB[bk]], kt == 7)
                        rt_, Rrt_ = relu_t[n % 2], Rrelu[n % 2]
                        act(rt_, banks[bk][:, 0:512], AF.Relu, [RB[bk]], [Rrt_])
                        act(uT[b][:, s_, blk], rt_, AF.Square, [Rrt_], [RuT[b]] + ([Rwup0] if c == 1 else []))

            def down(c):
                b = c % 2
                n = 0
                for tti in range(16):
                    tok = slice(128 * tti, 128 * (tti + 1))
                    for half in range(2):
                        bk = 4 + n % 4; n += 1
                        cs = slice(512 * half, 512 * (half + 1))
                        for s_ in range(4):
                            mm(banks[bk][:, 0:512], uT[b][:, s_, tok], wdn[b][:, s_, cs], s_ == 0, s_ == 3,
                               [RuT[b], Rwdn[b]], [RB[bk]], s_ == 3)
                        tt("dve", x1[:, tti, cs], banks[bk][:, 0:512], x1[:, tti, cs], ALU.add, [RB[bk], Rx1[tti]], [Rx1[tti]])

            for c in range(8):
                if c + 1 < 8:
                    load_up(c + 1)
                up(c)
                if c == 0:
                    dump("uT0", uT[0][:, 0, :], [RuT[0]])
                if c > 0:
                    down(c - 1)
                    if c + 1 < 8:
                        load_dn(c + 1)
            down(7)
            dump("x2_0", x1[:, 0, :], Rx1)

            for tti in range(16):
                Rst_ = S.res("fst")
                Ro_ = S.res("out%d" % tti)
                act(sqj2, x1[:, tti, :], AF.Square, [Rx1[tti]], [Rsqj2, Rst_], accum_out=ss[:, 16 + tti:17 + tti])
                act(lnv[:, 16 + tti:17 + tti], ss[:, 16 + tti:17 + tti], AF.Ln, [Rst_], [Rst_], scale=1.0 / 1024, bias=EPS)
                act(rs[:, 16 + tti:17 + tti], lnv[:, 16 + tti:17 + tti], AF.Exp, [Rst_], [Rst_], scale=-0.5)
                stt(x1[:, tti, :], x1[:, tti, :], rs[:, 16 + tti:17 + tti], gfin, ALU.mult, ALU.mult,
                    [Rx1[tti], Rst_, Rgfin], [Rx1[tti]])
                dma("sp", out_d[128 * tti:128 * (tti + 1), :], x1[:, tti, :], [Rx1[tti]], [Ro_])

        except _Stop:
            pass
        S.finish()
        S.emit()
    return nc


def _t5_bucket_np(rel):
    nb = 16
    max_exact = 8
    ret = np.where(rel > 0, nb, 0)
    n = np.abs(rel)
    nf = np.maximum(n, 1).astype(np.float32)
    large = max_exact + (np.log(nf / np.float32(max_exact)) / np.float32(math.log(1024 / max_exact))
                         * np.float32(nb - max_exact)).astype(np.int32)
    large = np.minimum(large, nb - 1)
    return ret + np.where(n < max_exact, n, large)


def _bias_index():
    i = np.arange(128)[:, None]
    jq = np.arange(128)[None, :]
    idx = np.zeros((128, 3, 256), np.int64)
    for d, p in PATS:
        for kk, sh in enumerate((-64, 64)):
            rel = i - jq + sh
            valid = np.abs(rel) <= 64
            b = _t5_bucket_np(rel * d)
            idx[:, p, kk * 128:(kk + 1) * 128] = np.where(valid, b, 32)
    return idx


def make_in_maps(inputs):
    f32 = np.float32
    x = np.asarray(inputs["x"], f32)
    w_in = np.ascontiguousarray(np.asarray(inputs["w_in"], f32)[0])
    w_out = np.ascontiguousarray(np.asarray(inputs["w_out"], f32)[0])
    w_up = np.ascontiguousarray(np.asarray(inputs["w_up"], f32)[0])
    w_down = np.ascontiguousarray(np.asarray(inputs["w_down"], f32)[0])

    def colv(v, n):
        return np.asarray(v, f32).reshape(n, 128).T

    cw = np.asarray(inputs["conv_w"], f32)[0]
    cwl = cw.reshape(4, 4, 128).transpose(2, 1, 0).reshape(128, 16)
    pvec = np.concatenate([
        colv(inputs["attn_norm_g"][0], 8), cwl, colv(inputs["conv_b"][0], 4),
        colv(inputs["lru_ba_fwd"][0], 4), colv(inputs["lru_bx_fwd"][0], 4), colv(inputs["lru_lam_fwd"][0], 4),
        colv(inputs["lru_ba_bwd"][0], 4), colv(inputs["lru_bx_bwd"][0], 4), colv(inputs["lru_lam_bwd"][0], 4),
        colv(inputs["norm_rnn_g"][0], 4), colv(inputs["norm_attn_g"][0], 4), colv(inputs["mlp_norm_g"][0], 8),
    ], axis=1).astype(f32)
    assert pvec.shape == (128, 68)
    def blockdiag(w):
        o = np.zeros((4, 128, 128), f32)
        for ct in range(4):
            o[ct, 0:64, 0:64] = w[2 * ct]
            o[ct, 64:128, 64:128] = w[2 * ct + 1]
        return o

    gsets = {}
    for dname in ("fwd", "bwd"):
        gsets[dname] = (blockdiag(np.asarray(inputs["lru_wa_" + dname], f32)[0]),
                        blockdiag(np.asarray(inputs["lru_wx_" + dname], f32)[0]))
    gw = np.zeros((128, 16, 128), f32)
    for m, (dname, which) in enumerate((("fwd", 0), ("fwd", 1), ("bwd", 0), ("bwd", 1))):
        for ct in range(4):
            gw[:, m * 4 + ct, :] = gsets[dname][which][ct]
    gw = gw.reshape(128, 16 * 128)

    def slot_params(dname):
        taps = cw if dname == "fwd" else cw[::-1]
        cwl_s = taps.reshape(4, 4, 128).transpose(2, 1, 0).reshape(128, 16)
        p = np.concatenate([cwl_s, colv(inputs["conv_b"][0], 4), colv(inputs["lru_ba_" + dname][0], 4),
                            colv(inputs["lru_bx_" + dname][0], 4), colv(inputs["lru_lam_" + dname][0], 4)], axis=1)
        g = np.zeros((128, 8, 128), f32)
        for ct in range(4):
            g[:, ct, :] = gsets[dname][0][ct]
            g[:, 4 + ct, :] = gsets[dname][1][ct]
        return p.astype(f32), g
    sp_ = {d_: slot_params(d_) for d_ in ("fwd", "bwd")}
    rb = np.asarray(inputs["rel_bias"], f32)
    tbl = np.concatenate([rb, np.full((1, 8), NEG, f32)], axis=0)
    idx = _bias_index()
    tb = tbl[idx]
    tb = np.ascontiguousarray(tb.transpose(0, 1, 3, 2)).reshape(128, 3 * 8 * 256)
    ident = np.eye(128, dtype=f32)
    swp = np.zeros((128, 2, 128), f32)
    for m in range(64):
        swp[m + 64, 0, m] = 1.0
        swp[m, 1, m + 64] = 1.0
    swp = swp.reshape(128, 256)
    gfin = np.ascontiguousarray(np.broadcast_to(np.asarray(inputs["final_norm_g"], f32)[None, :], (128, 1024)))
    in_maps = []
    for c in range(NCORES):
        b, j = c // 4, c % 4
        start = 2048 * j
        xf = np.zeros((4096, 1024), f32)
        lo, hi = start - 1024, start + 3072
        slo, shi = max(lo, 0), min(hi, 8192)
        xf[slo - lo:shi - lo] = x[b, slo:shi]
        val = np.zeros((128, 3, 192), f32)
        val[:, :, 0:64] = 1.0
        val[:, :, 128:192] = 1.0
        if j == 0:
            val[0:64, 1, :] = 0.0
        if j == 3:
            val[64:128, 2, :] = 0.0
        slots = [("fwd", k) for k in range(j)] + [("bwd", k) for k in range(3, j, -1)]
        xs = np.zeros((3, 2176, 1024), f32)
        v = np.arange(-64, 2112)
        for s_, (dname, k) in enumerate(slots):
            tok = (2048 * k + v) if dname == "fwd" else (2048 * k + 2046 - v)
            ok = (tok >= 0) & (tok < 8192)
            xs[s_, ok] = x[b, tok[ok]]
        psl = np.concatenate([sp_[dname][0] for dname, _ in slots], axis=1)
        gws = np.concatenate([sp_[dname][1] for dname, _ in slots], axis=1).reshape(128, 24 * 128)
        smask = np.zeros((128, 16), f32)
        for s_ in range(1, 3):
            if slots[s_][0] == slots[s_ - 1][0]:
                smask[:, s_] = 1.0
        if j >= 1:
            smask[:, 3 + j - 1] = 1.0
        if j < 3:
            smask[:, 6 + 2] = 1.0
        in_maps.append({
            "xf": xf, "w_in": w_in, "w_out": w_out, "w_up": w_up, "w_down": w_down, "pvec": pvec, "gw": gw,
            "tb": tb, "val": val.reshape(128, 576), "ident": ident, "gfin": gfin, "swp": swp,
            "xs": xs.reshape(3 * 2176, 1024), "psl": np.ascontiguousarray(psl), "gws": np.ascontiguousarray(gws), "smask": smask,
        })
    return in_maps


_NC_CACHE = {}


def kernel(**inputs):
    in_maps = make_in_maps(inputs)
    if "nc" not in _NC_CACHE:
        _NC_CACHE["nc"] = build()
    res = run_bass_kernel_spmd(_NC_CACHE["nc"], in_maps, core_ids=list(range(NCORES)))
    out = np.zeros((2, 8192, 1024), np.float32)
    for c in range(NCORES):
        b, j = c // 4, c % 4
        out[b, 2048 * j:2048 * (j + 1)] = res.results[c]["out"]
    return out
```
